# Optimizing a Trainium2 kernel written in Bass

```python
import jax, jax.numpy as jnp
from jax import lax
import numpy as np

D_MODEL = 1024
BATCH = 16
SEQ = 2048
DEPTH = 1
DEC_BATCH = 32
DEC_SEQ = 1
PAST_LEN = 16384
PAGE_SIZE = 128

SB_HEADS = 8
SB_HEAD_DIM = 64
SB_WIDTH = SB_HEADS * SB_HEAD_DIM
SB_BLOCK = 128
SB_BIAS_INIT = -8.0
GLA_HEADS = 4
GLA_HEAD_K = 64
GLA_HEAD_V = 128
GLA_K_WIDTH = GLA_HEADS * GLA_HEAD_K
GLA_V_WIDTH = GLA_HEADS * GLA_HEAD_V
GLA_GATE_RANK = 16
GLA_TAU = 16.0
GLA_CHUNK = 64
D_FF = 4 * D_MODEL
ALPHA = (2.0 * DEPTH) ** 0.25
BETA = (8.0 * DEPTH) ** -0.25
LN_EPS = 1e-5
GN_EPS = 1e-6
IN_SPLITS = (SB_WIDTH, SB_WIDTH, SB_WIDTH, GLA_K_WIDTH, GLA_K_WIDTH, GLA_V_WIDTH, GLA_V_WIDTH,
             GLA_GATE_RANK, D_MODEL, D_MODEL)
N_IN = 3 * SB_WIDTH + 2 * GLA_K_WIDTH + 2 * GLA_V_WIDTH + GLA_GATE_RANK + 2 * D_MODEL

kernel_name = "stickbreak_gla_deepnorm_adaln_step"


def _ln_stats(x):
    xf = x.astype(jnp.float32)
    mu = jnp.mean(xf, axis=-1, keepdims=True)
    var = jnp.mean(jnp.square(xf - mu), axis=-1, keepdims=True)
    return (xf - mu) * lax.rsqrt(var + LN_EPS)


def layer_norm(x, g, b):
    return (_ln_stats(x) * g + b).astype(x.dtype)


def modulate(x, shift, scale):
    return (_ln_stats(x) * (1.0 + scale) + shift).astype(x.dtype)


def sb_attend(q, k, v, bias, q_offset):
    B, Tq, H, d = q.shape
    Tk = k.shape[1]
    blk = SB_BLOCK if Tq % SB_BLOCK == 0 else Tq
    nb = Tq // blk
    qb = jnp.moveaxis(q.reshape(B, nb, blk, H, d), 1, 0)
    kpos = jnp.arange(Tk)
    scale = d ** -0.5
    bias_f = bias.astype(jnp.float32)[None, :, None, None]

    def one_block(args):
        qi, i = args
        z = jnp.einsum("bqhd,bkhd->bhqk", qi, k, preferred_element_type=jnp.float32) * scale + bias_f
        qpos = q_offset + i * blk + jnp.arange(blk)
        mask = kpos[None, :] < qpos[:, None]
        log_keep = jnp.where(mask, -jax.nn.softplus(z), 0.0)
        rest = lax.cumsum(log_keep, axis=3, reverse=True) - log_keep
        w = jnp.where(mask, jnp.exp(jax.nn.log_sigmoid(z) + rest), 0.0)
        return jnp.einsum("bhqk,bkhd->bqhd", w, v.astype(jnp.float32)).astype(q.dtype)

    o = lax.map(one_block, (qb, jnp.arange(nb)))
    return jnp.moveaxis(o, 0, 1).reshape(B, Tq, H * d)


def gla_mix(q, k, v, log_a, S0):
    B, T, H, dk = q.shape
    dv = v.shape[-1]
    C = GLA_CHUNK if T % GLA_CHUNK == 0 else T
    N = T // C
    f32 = jnp.float32
    q = (q.astype(f32) * dk ** -0.5).reshape(B, N, C, H, dk)
    k = k.astype(f32).reshape(B, N, C, H, dk)
    v = v.astype(f32).reshape(B, N, C, H, dv)
    b = jnp.cumsum(log_a.reshape(B, N, C, H, dk), axis=2)
    b_last = b[:, :, -1:]
    q_dec = q * jnp.exp(b)
    k_inv = k * jnp.exp(-b)
    k_end = k * jnp.exp(b_last - b)
    causal = jnp.tril(jnp.ones((C, C), dtype=bool))
    att = jnp.where(causal, jnp.einsum("bnihk,bnjhk->bnhij", q_dec, k_inv), 0.0)
    o_intra = jnp.einsum("bnhij,bnjhv->bnihv", att, v)
    U = jnp.einsum("bnjhk,bnjhv->nbhkv", k_end, v)
    decay = jnp.moveaxis(jnp.exp(b_last[:, :, 0]), 1, 0)

    def step(S, inp):
        dcy, u = inp
        return dcy[..., None] * S + u, S

    S_T, S_prev = lax.scan(step, S0.astype(f32), (decay, U))
    o_inter = jnp.einsum("bnihk,nbhkv->bnihv", q_dec, S_prev)
    return (o_intra + o_inter).reshape(B, T, H, dv), S_T


def trunk_layer(x, c, past_k, past_v, S0, w_ada, b_ada, w_in, sb_bias, w_gla_g2, b_gla_g, gla_norm_g,
                w_up_a, w_up_b, w_o, ln1_g, ln1_b, w_ff1, w_ff2, ln2_g, ln2_b):
    B, T, _ = x.shape
    P = past_k.shape[1]
    ada = jax.nn.silu(c) @ w_ada + b_ada
    sh1, sc1, gt1, sh2, sc2, gt2 = [a[:, None, :] for a in jnp.split(ada, 6, axis=-1)]

    h = modulate(x, sh1, sc1)
    split_at = [int(s) for s in np.cumsum(IN_SPLITS)[:-1]]
    sb_q, sb_k, sb_v, g_q, g_k, g_v, g_r, g_low, br_a, br_b = jnp.split(h @ w_in, split_at, axis=-1)

    def heads(t, n):
        return t.reshape(B, T, n, -1)

    k_sb = heads(sb_k, SB_HEADS)
    v_sb = heads(sb_v, SB_HEADS)
    keys = jnp.concatenate([past_k.astype(k_sb.dtype), k_sb], axis=1)
    vals = jnp.concatenate([past_v.astype(v_sb.dtype), v_sb], axis=1)
    o_sb = sb_attend(heads(sb_q, SB_HEADS), keys, vals, sb_bias, P)

    log_a = jax.nn.log_sigmoid((g_low @ w_gla_g2 + b_gla_g).astype(jnp.float32)) / GLA_TAU
    o_g, S_T = gla_mix(heads(g_q, GLA_HEADS), heads(g_k, GLA_HEADS), heads(g_v, GLA_HEADS),
                       heads(log_a, GLA_HEADS), S0)
    o_g = o_g * lax.rsqrt(jnp.mean(jnp.square(o_g), axis=-1, keepdims=True) + GN_EPS) * gla_norm_g
    o_g = (o_g * jax.nn.silu(heads(g_r, GLA_HEADS).astype(jnp.float32))).reshape(B, T, GLA_V_WIDTH)
    o_g = o_g.astype(x.dtype)

    merged = jax.nn.sigmoid(br_a) * (o_sb @ w_up_a) + jax.nn.sigmoid(br_b) * (o_g @ w_up_b)
    x1 = layer_norm(ALPHA * x + gt1 * (merged @ w_o), ln1_g, ln1_b)

    h2 = modulate(x1, sh2, sc2)
    f = jnp.square(jax.nn.relu(h2 @ w_ff1)) @ w_ff2
    x2 = layer_norm(ALPHA * x1 + gt2 * f, ln2_g, ln2_b)
    return x2, k_sb, v_sb, S_T.astype(S0.dtype)


def setup_inputs(seed: int = 0) -> dict:
    key = jax.random.key(seed)
    ks = jax.random.split(key, 24)
    n_pages = PAST_LEN // PAGE_SIZE
    n_phys = (DEC_BATCH * n_pages * 5) // 4

    def nrm(k, shape, scale):
        return jax.random.normal(k, shape, jnp.float32) * scale

    page_table = jax.random.permutation(ks[0], n_phys)[: DEC_BATCH * n_pages]
    page_table = page_table.reshape(DEC_BATCH, n_pages).astype(jnp.int32)
    L = DEPTH
    return {
        "x_prompt": nrm(ks[1], (BATCH, SEQ, D_MODEL), 1.0),
        "x_sample": nrm(ks[2], (DEC_BATCH, DEC_SEQ, D_MODEL), 1.0),
        "cache_k": nrm(ks[3], (L, n_phys, PAGE_SIZE, SB_HEADS, SB_HEAD_DIM), 1.0),
        "cache_v": nrm(ks[4], (L, n_phys, PAGE_SIZE, SB_HEADS, SB_HEAD_DIM), 1.0),
        "state_gla": nrm(ks[5], (L, DEC_BATCH, GLA_HEADS, GLA_HEAD_K, GLA_HEAD_V), 1.0),
        "page_table": page_table,
        "c_prompt": nrm(ks[6], (BATCH, D_MODEL), 1.0),
        "c_sample": nrm(ks[7], (DEC_BATCH, D_MODEL), 1.0),
        "w_ada": nrm(ks[8], (L, D_MODEL, 6 * D_MODEL), 0.5 * D_MODEL ** -0.5),
        "b_ada": nrm(ks[9], (L, 6 * D_MODEL), 0.01),
        "w_in": nrm(ks[10], (L, D_MODEL, N_IN), D_MODEL ** -0.5),
        "sb_bias": SB_BIAS_INIT + nrm(ks[23], (L, SB_HEADS), 0.1),
        "w_gla_g2": nrm(ks[11], (L, GLA_GATE_RANK, GLA_K_WIDTH), GLA_GATE_RANK ** -0.5),
        "b_gla_g": nrm(ks[12], (L, GLA_K_WIDTH), 0.1),
        "gla_norm_g": 1.0 + nrm(ks[13], (L, GLA_HEAD_V), 0.02),
        "w_up_a": nrm(ks[14], (L, SB_WIDTH, D_MODEL), SB_WIDTH ** -0.5),
        "w_up_b": nrm(ks[15], (L, GLA_V_WIDTH, D_MODEL), GLA_V_WIDTH ** -0.5),
        "w_o": nrm(ks[16], (L, D_MODEL, D_MODEL), BETA * D_MODEL ** -0.5),
        "ln1_g": 1.0 + nrm(ks[17], (L, D_MODEL), 0.02),
        "ln1_b": nrm(ks[18], (L, D_MODEL), 0.01),
        "w_ff1": nrm(ks[19], (L, D_MODEL, D_FF), D_MODEL ** -0.5),
        "w_ff2": nrm(ks[20], (L, D_FF, D_MODEL), BETA * D_FF ** -0.5),
        "ln2_g": 1.0 + nrm(ks[21], (L, D_MODEL), 0.02),
        "ln2_b": nrm(ks[22], (L, D_MODEL), 0.01),
    }


def reference(x_prompt, x_sample, cache_k, cache_v, state_gla, page_table, c_prompt, c_sample,
              w_ada, b_ada, w_in, sb_bias, w_gla_g2, b_gla_g, gla_norm_g, w_up_a, w_up_b, w_o,
              ln1_g, ln1_b, w_ff1, w_ff2, ln2_g, ln2_b):
    n_pages = page_table.shape[1]
    past = n_pages * cache_k.shape[2]
    dec_b = x_sample.shape[0]
    pb = x_prompt.shape[0]
    yp, ys = x_prompt, x_sample
    kp_l, vp_l, sp_l, ks_l, vs_l, ss_l = [], [], [], [], [], []
    for l in range(DEPTH):
        wl = (w_ada[l], b_ada[l], w_in[l], sb_bias[l], w_gla_g2[l], b_gla_g[l], gla_norm_g[l], w_up_a[l],
              w_up_b[l], w_o[l], ln1_g[l], ln1_b[l], w_ff1[l], w_ff2[l], ln2_g[l], ln2_b[l])
        empty = jnp.zeros((pb, 0, SB_HEADS, SB_HEAD_DIM), x_prompt.dtype)
        s0 = jnp.zeros((pb, GLA_HEADS, GLA_HEAD_K, GLA_HEAD_V), state_gla.dtype)
        yp, kp, vp, sp = trunk_layer(yp, c_prompt, empty, empty, s0, *wl)
        pk = cache_k[l][page_table].reshape(dec_b, past, SB_HEADS, SB_HEAD_DIM)
        pv = cache_v[l][page_table].reshape(dec_b, past, SB_HEADS, SB_HEAD_DIM)
        ys, kss, vss, sss = trunk_layer(ys, c_sample, pk, pv, state_gla[l], *wl)
        kp_l.append(kp); vp_l.append(vp); sp_l.append(sp)
        ks_l.append(kss); vs_l.append(vss); ss_l.append(sss)
    new_k_prompt = jnp.stack(kp_l)
    new_v_prompt = jnp.stack(vp_l)
    new_state_prompt = jnp.stack(sp_l)
    new_k_sample = jnp.stack(ks_l)
    new_v_sample = jnp.stack(vs_l)
    new_state_sample = jnp.stack(ss_l)
    return (yp, ys, new_k_prompt, new_v_prompt, new_state_prompt, new_k_sample, new_v_sample, new_state_sample)
```

```python
import numpy as np
import concourse.bass as bass
import concourse.mybir as mybir
from concourse.bass_utils import run_bass_kernel_spmd

F32 = mybir.dt.float32
BF16 = mybir.dt.bfloat16
I32 = mybir.dt.int32
AF = mybir.ActivationFunctionType
ALU = mybir.AluOpType
AX = mybir.AxisListType

D = 1024
NIN = 5136
NCORES = 8
ALPHA = 2.0 ** 0.25
LN_EPS = 1e-5
GN_EPS = 1e-6
N1A = 3088


class StopBuild(Exception):
    pass


class Cfg:
    def __init__(self, pb=2, seq=2048, sb=4, pages=128, nphys=5120):
        self.pb, self.seq, self.sb, self.pages, self.nphys = pb, seq, sb, pages, nphys
        self.ntok = pb * seq


class Buf:
    __slots__ = ("w", "r")

    def __init__(self):
        self.w = None
        self.r = {}


class Eng:
    def __init__(self, name):
        self.name = name
        self.ops = []
        self.count = 0
        self.seen = {}
        self.sem = None


class DSem:
    def __init__(self, i):
        self.name = f"dma{i}"
        self.sem = None
        self.count = 0


class T:
    def __init__(self, t, buf=None):
        self.t = t
        self.b = buf or Buf()

    def __getitem__(self, k):
        return self.t[k]


class Prog:
    def __init__(self, nc):
        self.nc = nc
        self.pe, self.act, self.dve, self.pool, self.sp = (Eng(n) for n in ("pe", "act", "dve", "pool", "sp"))
        self.engs = [self.pe, self.act, self.dve, self.pool, self.sp]
        self.dsems = [DSem(i) for i in range(40)]
        self.dnext = 0
        self.out_toks = []

    def _waits(self, eng, reads, writes, extra=()):
        waits = {}

        def need(tok):
            if tok is None:
                return
            e, v = tok
            if waits.get(e, 0) < v:
                waits[e] = v
        for b in reads:
            need(b.b.w)
        for b in writes:
            need(b.b.w)
            for e, v in b.b.r.items():
                need((e, v))
        for tok in extra:
            need(tok)
        wl = []
        for e, v in waits.items():
            if e is eng and eng is self.pe:
                continue
            if eng.seen.get(e, 0) < v:
                eng.seen[e] = v
                wl.append((e, v))
        return wl

    def op(self, eng, fn, reads=(), writes=()):
        wl = self._waits(eng, reads, writes)
        eng.count += 1
        tok = (eng, eng.count)
        eng.ops.append((wl, fn, eng, 1))
        for b in reads:
            b.b.r[eng] = eng.count
        for b in writes:
            b.b.w = tok
            b.b.r = {}
        return tok

    def dma(self, q, fn, reads=(), writes=(), is_out=False):
        ds = self.dsems[self.dnext % len(self.dsems)]
        self.dnext += 1
        extra = [(ds, ds.count)] if ds.count else []
        wl = self._waits(q, reads, writes, extra)
        ds.count += 16
        tok = (ds, ds.count)
        q.ops.append((wl, fn, ds, 16))
        for b in reads:
            b.b.r[ds] = ds.count
        for b in writes:
            b.b.w = tok
            b.b.r = {}
        if is_out:
            self.out_toks.append(tok)
        return tok

    def barrier(self):
        targets = [(e, e.count) for e in self.engs if e.count > 0] + [(d, d.count) for d in self.dsems if d.count > 0]
        for eng in self.engs:
            wl = []
            for e, v in targets:
                if eng.seen.get(e, 0) < v:
                    eng.seen[e] = v
                    wl.append((e, v))
            eng.count += 1
            eng.ops.append((wl, lambda h: h.nop(), eng, 1))

    def emit(self, block):
        nc = self.nc
        fin = {}
        for e, v in self.out_toks:
            fin[e] = max(fin.get(e, 0), v)

        def run(eng, h):
            for wl, fn, semobj, inc in eng.ops:
                for e, v in wl:
                    h.wait_ge(e.sem, v)
                fn(h).then_inc(semobj.sem, inc)
            if eng is self.sp:
                for e, v in fin.items():
                    h.wait_ge(e.sem, v)

        block.sync(lambda h: run(self.sp, h))
        block.tensor(lambda h: run(self.pe, h))
        block.scalar(lambda h: run(self.act, h))
        block.vector(lambda h: run(self.dve, h))
        block.gpsimd(lambda h: run(self.pool, h))


def build(cfg):
    nc = bass.Bass("TRN2", target_bir_lowering=False)
    P = Prog(nc)
    pe, act, dve, pool, sp = P.pe, P.act, P.dve, P.pool, P.sp
    PB, SEQ, SB, PAGES, NPHYS, NTOK = cfg.pb, cfg.seq, cfg.sb, cfg.pages, cfg.nphys, cfg.ntok
    NR = PB + SB
    NT_SEQ = SEQ // 128
    NCH = SEQ // 512
    NPG = SB * PAGES

    def din(name, shape, dt=F32):
        return nc.dram_tensor(name, list(shape), dt, kind="ExternalInput").ap()

    def dout(name, shape, dt=F32):
        return nc.dram_tensor(name, list(shape), dt, kind="ExternalOutput").ap()

    x_p = din("x_p", [NTOK, D])
    x_s = din("x_s", [SB, D])
    cache_k = din("cache_k", [NPHYS * 128, 512])
    cache_v = din("cache_v", [NPHYS * 128, 512])
    state = din("state", [SB, 4, 64, 128])
    ptab = din("ptab", [1, NPG], I32)
    c_all = din("c_all", [NR, D])
    w_ada = din("w_ada", [D, 6 * D])
    b_ada = din("b_ada", [1, 6 * D])
    w_in = din("w_in", [D, NIN])
    sb_bias = din("sb_bias", [1, 8])
    w_g2 = din("w_g2", [16, 256])
    b_g = din("b_g", [1, 256])
    normg = din("normg", [1, 128])
    w_upa = din("w_upa", [512, D])
    w_upb = din("w_upb", [512, D])
    w_o = din("w_o", [D, D])
    ln1g = din("ln1g", [1, D])
    ln1b = din("ln1b", [1, D])
    w_ff1 = din("w_ff1", [D, 4 * D])
    w_ff2 = din("w_ff2", [4 * D, D])
    ln2g = din("ln2g", [1, D])
    ln2b = din("ln2b", [1, D])

    y_p = dout("y_p", [NTOK, D])
    y_s = dout("y_s", [SB, D])
    nk_p = dout("nk_p", [NTOK, 512])
    nv_p = dout("nv_p", [NTOK, 512])
    ns_p = dout("ns_p", [PB, 4, 64, 128])
    nk_s = dout("nk_s", [SB, 512])
    nv_s = dout("nv_s", [SB, 512])
    ns_s = dout("ns_s", [SB, 4, 64, 128])

    dbgk = {}
    sc_osb = nc.dram_tensor("sc_osb", [8, 64, NTOK], BF16, **dbgk).ap()
    sc_og = nc.dram_tensor("sc_og", [4, 128, NTOK], BF16, **dbgk).ap()
    sc_x1 = nc.dram_tensor("sc_x1", [NTOK, D], F32, **dbgk).ap()
    sc_osb_s = nc.dram_tensor("sc_osb_s", [8, 64, SB], BF16, **dbgk).ap()
    sc_og_s = nc.dram_tensor("sc_og_s", [4, 128, SB], BF16, **dbgk).ap()
    sc_x1_s = nc.dram_tensor("sc_x1_s", [SB, D], F32, **dbgk).ap()
    d_osb, d_og, d_x1, d_osb_s, d_og_s, d_x1_s = (Buf() for _ in range(6))

    class DB:
        def __init__(self, b):
            self.b = b

    _n = [0]

    def sbp(shape, dt=F32, name=None):
        _n[0] += 1
        return T(nc.alloc_sbuf_tensor(name or f"t{_n[0]}", list(shape), dt))

    RAWN = 82000
    TOPN = 8192
    rawh = []
    off = [0]

    limit = [RAWN]

    def sb(shape, dt=F32, name=None, at=None):
        shape = list(shape)
        n = int(np.prod(shape[1:]))
        ne = n * (2 if dt in (F32, I32) else 1)
        if at is None:
            a = (off[0] + 15) // 16 * 16
            off[0] = a + ne
            assert off[0] <= limit[0], (off[0], name)
        else:
            a = at
            assert a + ne <= RAWN
        v = rawh[0][0:shape[0], a:a + ne]
        if dt != BF16:
            v = v.bitcast(dt)
        if len(shape) == 3:
            v = v.rearrange("p (a b) -> p a b", a=shape[1])
        elif len(shape) == 4:
            v = v.rearrange("p (a b c) -> p a b c", a=shape[1], b=shape[2])
        elif len(shape) == 5:
            v = v.rearrange("p (a b c d) -> p a b c d", a=shape[1], b=shape[2], c=shape[3])
        return T(v)

    def ps(shape, dt=F32, name=None):
        _n[0] += 1
        return T(nc.alloc_psum_tensor(name or f"p{_n[0]}", list(shape), dt))

    ident_f = sbp([128, 128], F32)
    ident_b = sbp([128, 128], BF16)
    iota_p = sbp([128, 1], F32)
    iota_f = sbp([128, 512], F32)
    tri01 = sbp([128, 128], F32)
    negtri = sbp([128, 128], BF16)
    negones = sbp([128, 128], BF16)
    ones_b = sbp([128, 128], BF16)
    negm = sbp([128, 128], BF16)
    blktri = sbp([128, 128], F32)
    ind2 = sbp([128, 2], F32)
    umat = sbp([128, 128], F32)
    eye4 = sbp([128, 4, 4], F32)
    sel4 = sbp([4, 4, 128], F32)
    bd8 = sbp([8, 512], F32)
    msk = sbp([8, 512], F32)
    ones_f = sbp([128, 128], F32)
    tmpc = sbp([128, 512], F32)
    cpc = sbp([128, 1], F32)
    epsc = sbp([128, 1], F32)
    adaT = sbp([128, 48, NR], F32)
    bgbc = sbp([128, 256], F32)
    ngbc = sbp([128, 128], F32)
    biasbc = sbp([128, 8], F32)
    wg2 = sbp([16, 256], F32)
    xt = [sbp([128, D], F32) for _ in range(2)]
    xn = sbp([128, D], BF16)
    def stat_set():
        return (sbp([128, 12], F32), sbp([128, 2], F32), sbp([128, 1], F32), sbp([128, 1], F32), sbp([128, 1], F32))
    STA = stat_set()
    STB = stat_set()
    hT2 = sbp([128, 8, 128], BF16)
    osb2 = sbp([64, 8, 128], BF16)
    ogl2 = sbp([128, 4, 128], BF16)
    hT = sbp([128, 8, 128], BF16)
    hTs = sbp([128, 8, SB], BF16)
    tmp84 = sbp([128, 8, SB], F32)
    big = sbp([128, D], F32)
    big2 = sbp([128, D], F32)
    osb = sbp([64, 8, 128], BF16)
    ogl = sbp([128, 4, 128], BF16)
    ogT = sbp([128, 4, 128], BF16)

    rawh.append(nc.alloc_sbuf_tensor("raw", [128, RAWN], BF16))
    lng = sb([128, D], F32, at=RAWN - TOPN)
    lnb = sb([128, D], F32, at=RAWN - TOPN + 2048)
    gtbc = sb([128, D], F32, at=RAWN - TOPN + 4096)
    gts = sb([SB, D], F32, at=RAWN - TOPN + 6144)
    w1a_t = sb([128, 8, N1A], BF16)
    w1a = w1a_t.t
    W1A = w1a_t
    mark_1a = off[0]
    wada_t = [sb([128, 8, 512], BF16) for _ in range(2)]
    wada = [w.t for w in wada_t]
    WADA = wada_t
    cin = sb([NR, D], F32)
    csl = sb([NR, D], BF16)
    cT = sb([128, 8, NR], BF16)
    adag = sb([NR, 512], F32)
    badag = sb([NR, 512], F32)

    pm = [ps([128, 512], F32) for _ in range(2)]
    ptp = ps([128, 1024], BF16)
    pz = [ps([128, 512], F32) for _ in range(2)]
    po = ps([128, 512], F32)
    pg1 = ps([128, 512], F32)
    pg2 = ps([128, 512], F32)

    def V(fn, reads, writes):
        return P.op(dve, fn, reads, writes)

    def ACT(fn, reads, writes):
        return P.op(act, fn, reads, writes)

    def PE(fn, reads, writes):
        return P.op(pe, fn, reads, writes)

    def POOL(fn, reads, writes):
        return P.op(pool, fn, reads, writes)

    def LD(out_ap, in_ap, writes, reads=(), q=None):
        return P.dma(q or sp, lambda h: h.dma_start(out=out_ap, in_=in_ap), reads, writes)

    def ST(out_ap, in_ap, reads, writes=(), is_out=True):
        return P.dma(sp, lambda h: h.dma_start(out=out_ap, in_=in_ap), reads, writes, is_out=is_out)

    def CASTLD(out_ap, in_ap, writes):
        return P.dma(pool, lambda h: h.dma_start(out=out_ap, in_=in_ap), (), writes)

    STOP = 99.0

    def stop_at(k):
        if STOP <= k:
            raise StopBuild()

    try:
        POOL(lambda h: h.iota(iota_p[:], pattern=[[0, 1]], base=0, channel_multiplier=1,
                              allow_small_or_imprecise_dtypes=True), [], [iota_p])
        POOL(lambda h: h.iota(iota_f[:], pattern=[[1, 512]], base=0, channel_multiplier=0,
                              allow_small_or_imprecise_dtypes=True), [], [iota_f])
        POOL(lambda h: h.iota(eye4[:], pattern=[[1, 4], [-1, 4]], base=0, channel_multiplier=0,
                              allow_small_or_imprecise_dtypes=True), [], [eye4])
        POOL(lambda h: h.iota(sel4[:], pattern=[[1, 4], [0, 128]], base=0, channel_multiplier=-1,
                              allow_small_or_imprecise_dtypes=True), [], [sel4])
        POOL(lambda h: h.iota(bd8[:], pattern=[[1, 512]], base=0, channel_multiplier=-64,
                              allow_small_or_imprecise_dtypes=True), [], [bd8])
        V(lambda h: h.memset(ones_f[:], 1.0), [], [ones_f])
        V(lambda h: h.memset(negones[:], -1.0), [], [negones])
        V(lambda h: h.memset(ones_b[:], 1.0), [], [ones_b])
        V(lambda h: h.memset(epsc[:], LN_EPS), [], [epsc])
        V(lambda h: h.tensor_scalar(out=eye4[:], in0=eye4[:], scalar1=0.0, scalar2=None, op0=ALU.is_equal), [eye4], [eye4])
        V(lambda h: h.tensor_scalar(out=sel4[:], in0=sel4[:], scalar1=0.0, scalar2=None, op0=ALU.is_equal), [sel4], [sel4])
        V(lambda h: h.tensor_scalar(out=msk[:], in0=bd8[:], scalar1=0.0, scalar2=None, op0=ALU.is_ge), [bd8], [msk])
        V(lambda h: h.tensor_scalar(out=bd8[:], in0=bd8[:], scalar1=64.0, scalar2=None, op0=ALU.is_lt), [bd8], [bd8])
        V(lambda h: h.tensor_tensor(out=bd8[:], in0=bd8[:], in1=msk[:], op=ALU.mult), [bd8, msk], [bd8])
        F128 = iota_f[:, 0:128]
        V(lambda h: h.tensor_scalar(out=ident_f[:], in0=F128, scalar1=iota_p[:, 0:1], scalar2=None, op0=ALU.is_equal),
          [iota_f, iota_p], [ident_f])
        V(lambda h: h.tensor_copy(out=ident_b[:], in_=ident_f[:]), [ident_f], [ident_b])
        V(lambda h: h.tensor_scalar(out=tri01[:], in0=F128, scalar1=iota_p[:, 0:1], scalar2=None, op0=ALU.is_le),
          [iota_f, iota_p], [tri01])
        V(lambda h: h.tensor_scalar(out=negtri[:], in0=tri01[:], scalar1=-1.0, scalar2=None, op0=ALU.mult), [tri01], [negtri])
        V(lambda h: h.tensor_scalar(out=negm[:], in0=tri01[:], scalar1=-30000.0, scalar2=None, op0=ALU.mult), [tri01], [negm])
        V(lambda h: h.tensor_scalar(out=umat[:], in0=F128, scalar1=iota_p[:, 0:1], scalar2=None, op0=ALU.is_lt),
          [iota_f, iota_p], [umat])
        V(lambda h: h.tensor_scalar(out=cpc[:], in0=iota_p[:], scalar1=64.0, scalar2=None, op0=ALU.is_ge), [iota_p], [cpc])
        V(lambda h: h.tensor_scalar(out=tmpc[:, 0:128], in0=F128, scalar1=64.0, scalar2=None, op0=ALU.is_ge), [iota_f], [tmpc])
        V(lambda h: h.tensor_scalar(out=tmpc[:, 0:128], in0=tmpc[:, 0:128], scalar1=cpc[:, 0:1], scalar2=None, op0=ALU.is_equal),
          [tmpc, cpc], [tmpc])
        V(lambda h: h.tensor_scalar(out=blktri[:], in0=F128, scalar1=iota_p[:, 0:1], scalar2=None, op0=ALU.is_ge),
          [iota_f, iota_p], [blktri])
        V(lambda h: h.tensor_tensor(out=blktri[:], in0=blktri[:], in1=tmpc[:, 0:128], op=ALU.mult), [blktri, tmpc], [blktri])
        V(lambda h: h.tensor_copy(out=ind2[:, 1:2], in_=cpc[:]), [cpc], [ind2])
        V(lambda h: h.tensor_scalar(out=ind2[:, 0:1], in0=cpc[:], scalar1=-1.0, scalar2=1.0, op0=ALU.mult, op1=ALU.add), [cpc], [ind2])

        LD(bgbc[:], b_g.partition_broadcast(128), [bgbc])
        LD(ngbc[:], normg.partition_broadcast(128), [ngbc])
        LD(biasbc[:], sb_bias.partition_broadcast(128), [biasbc])
        LD(wg2[:], w_g2, [wg2])

        LD(cin[:], c_all, [cin])
        ACT(lambda h: h.activation(out=csl[:], in_=cin[:], func=AF.Silu), [cin], [csl])
        for c in range(8):
            PE(lambda h, c=c: h.transpose(out=ptp[:, c * NR:(c + 1) * NR], in_=csl[:, c * 128:(c + 1) * 128], identity=ident_b[0:NR, 0:NR]),
               [csl, ident_b], [ptp])
        V(lambda h: h.tensor_copy(out=cT[:], in_=ptp[:, 0:8 * NR].rearrange("p (c r) -> p c r", c=8)), [ptp], [cT])
        w_ada_v = w_ada.rearrange("(k p) n -> p k n", p=128)
        for g in range(12):
            wb = g % 2
            CASTLD(wada[wb], w_ada_v[:, :, g * 512:(g + 1) * 512], [WADA[wb]])
            LD(badag[:], b_ada[:, g * 512:(g + 1) * 512].partition_broadcast(NR), [badag])
            for kc in range(8):
                PE(lambda h, kc=kc, wb=wb: h.matmul(pm[0][0:NR, :], lhsT=cT[:, kc, :], rhs=wada[wb][:, kc, :], start=(kc == 0), stop=(kc == 7)),
                   [cT, WADA[wb]], [pm[0]])
            V(lambda h: h.tensor_tensor(out=adag[:], in0=pm[0][0:NR, :], in1=badag[:], op=ALU.add), [pm[0], badag], [adag])
            for j in range(4):
                PE(lambda h, j=j: h.transpose(out=pg1[:, j * NR:(j + 1) * NR], in_=adag[:, j * 128:(j + 1) * 128], identity=ident_f[0:NR, 0:NR]),
                   [adag, ident_f], [pg1])
            V(lambda h, g=g: h.tensor_copy(out=adaT[:, 4 * g:4 * g + 4, :], in_=pg1[:, 0:4 * NR].rearrange("p (c r) -> p c r", c=4)),
              [pg1], [adaT])
        V(lambda h: h.tensor_scalar(out=adaT[:, 8:16, :], in0=adaT[:, 8:16, :], scalar1=1.0, scalar2=None, op0=ALU.add), [adaT], [adaT])
        V(lambda h: h.tensor_scalar(out=adaT[:, 32:40, :], in0=adaT[:, 32:40, :], scalar1=1.0, scalar2=None, op0=ALU.add), [adaT], [adaT])

        stop_at(0)
        w_in_v = w_in.rearrange("(k p) n -> p k n", p=128)
        for kc in range(8):
            for (a, b_) in ((0, 1544), (1544, N1A)):
                CASTLD(w1a[:, kc, a:b_], w_in_v[:, kc, a:b_], [W1A])
        P.barrier()
        off[0] = mark_1a
        kst = sb([128, 512], F32)
        vst = sb([128, 512], F32)
        glT = sb([16, 128], F32)
        pre = sb([128, 256], F32)
        lpos = sb([128, 256], F32)
        srg = sb([128, 512], F32)
        og_f = sb([128, 512], F32)
        og_b = sb([128, 512], BF16)
        ssq = sb([128, 4], F32)
        rinv = sb([128, 4], F32)
        mark_shared = off[0]
        qT = sb([128, 4, 512], BF16)
        kT = sb([128, 4, SEQ], BF16)
        vseq = sb([128, NT_SEQ, 512], BF16)
        e_t = [sb([128, 512], BF16) for _ in range(2)]
        sp_t = [sb([128, 512], BF16) for _ in range(2)]
        a_t = [sb([128, 512], BF16) for _ in range(2)]
        r_t = sb([128, 512], BF16)
        osT = sb([64, 512], BF16)
        eb = sb([128, 256], F32)
        einv = sb([128, 256], F32)
        qdec = sb([128, 256], BF16)
        kinv = sb([128, 256], BF16)
        qdT = sb([128, 2, 128], BF16)
        qdZ = sb([128, 2, 2, 128], BF16)
        V(lambda h: h.memset(qdZ[:].rearrange("p a b c -> p (a b c)"), 0.0), [], [qdZ])
        qdZ2 = sb([128, 2, 2, 2, 128], BF16)
        V(lambda h: h.memset(qdZ2[:].rearrange("p a b c d -> p (a b c d)"), 0.0), [], [qdZ2])
        kinvZ = sb([128, 2, 256], BF16)
        V(lambda h: h.memset(kinvZ[:].rearrange("p a b -> p (a b)"), 0.0), [], [kinvZ])
        qTz = sb([128, 2, 4, 512], BF16)
        V(lambda h: h.memset(qTz[:].rearrange("p a b c -> p (a b c)"), 0.0), [], [qTz])
        kiT = sb([128, 2, 128], BF16)
        attT = sb([128, 512], BF16)
        gvb = sb([128, 512], BF16)
        S32 = sb([128, 2, 128], F32)
        S16 = sb([128, 2, 128], BF16)
        dec = sb([128, 2, 2], F32)

        def ln_stats(src, n, S=None):
            st6, mv, lnv, rstd, nmr = S or STA
            V(lambda h: h.bn_stats(out=st6[0:n, 0:6], in_=src[0:n, 0:512]), [src], [st6])
            V(lambda h: h.bn_stats(out=st6[0:n, 6:12], in_=src[0:n, 512:1024]), [src], [st6])
            V(lambda h: h.bn_aggr(out=mv[0:n, :], in_=st6[0:n, :]), [st6], [mv])
            ACT(lambda h: h.activation(out=lnv[0:n, :], in_=mv[0:n, 1:2], func=AF.Ln, bias=epsc[0:n, :]), [mv, epsc], [lnv])
            ACT(lambda h: h.activation(out=rstd[0:n, :], in_=lnv[0:n, :], func=AF.Exp, scale=-0.5), [lnv], [rstd])
            V(lambda h: h.scalar_tensor_tensor(out=nmr[0:n, :], in0=mv[0:n, 0:1], scalar=-1.0, in1=rstd[0:n, :], op0=ALU.mult, op1=ALU.mult),
              [mv, rstd], [nmr])
            return rstd, nmr

        def ln_to_hT(src, n, r, sh_c, sc_c, hT=hT):
            rstd, nmr = ln_stats(src, n)
            V(lambda h: h.tensor_scalar(out=xn[0:n, :], in0=src[0:n, :], scalar1=rstd[0:n, :], scalar2=nmr[0:n, :], op0=ALU.mult, op1=ALU.add),
              [src, rstd, nmr], [xn])
            for c in range(8):
                PE(lambda h, c=c: h.transpose(out=ptp[:, c * 128:c * 128 + n], in_=xn[0:n, c * 128:(c + 1) * 128], identity=ident_b[0:n, 0:n]),
                   [xn, ident_b], [ptp])
            if r is not None:
                for c in range(8):
                    V(lambda h, c=c: h.tensor_scalar(out=hT[:, c, :], in0=ptp[:, c * 128:(c + 1) * 128],
                                                     scalar1=adaT[:, sc_c + c, r:r + 1], scalar2=adaT[:, sh_c + c, r:r + 1],
                                                     op0=ALU.mult, op1=ALU.add), [ptp, adaT], [hT])
                return hT, lambda kc: hT[:, kc, :]
            tpv = ptp[:, :].rearrange("p (c t) -> p c t", c=8)[:, :, 0:n]
            V(lambda h: h.tensor_tensor(out=tmp84[:], in0=tpv, in1=adaT[:, sc_c:sc_c + 8, PB:NR], op=ALU.mult), [ptp, adaT], [tmp84])
            V(lambda h: h.tensor_tensor(out=hTs[:], in0=tmp84[:], in1=adaT[:, sh_c:sh_c + 8, PB:NR], op=ALU.add), [tmp84, adaT], [hTs])
            return hTs, lambda kc: hTs[:, kc, :]

        def residual_ln(n, xres, pa, pb_, gt, dst, dst_reads_extra=()):
            V(lambda h: h.tensor_tensor(out=big[0:n, 0:512], in0=pa[0:n, :], in1=gt[0:n, 0:512], op=ALU.mult), [pa, gt], [big])
            V(lambda h: h.tensor_tensor(out=big[0:n, 512:1024], in0=pb_[0:n, :], in1=gt[0:n, 512:1024], op=ALU.mult), [pb_, gt], [big])
            V(lambda h: h.scalar_tensor_tensor(out=big[0:n, :], in0=xres[0:n, :], scalar=ALPHA, in1=big[0:n, :], op0=ALU.mult, op1=ALU.add),
              [xres, big], [big])
            rstd, nmr = ln_stats(big, n, STB)
            V(lambda h: h.tensor_scalar(out=big[0:n, :], in0=big[0:n, :], scalar1=rstd[0:n, :], scalar2=nmr[0:n, :], op0=ALU.mult, op1=ALU.add),
              [big, rstd, nmr], [big])
            V(lambda h: h.tensor_tensor(out=big[0:n, :], in0=big[0:n, :], in1=lng[0:n, :], op=ALU.mult), [big, lng], [big])
            V(lambda h: h.tensor_tensor(out=dst[0:n, :], in0=big[0:n, :], in1=lnb[0:n, :], op=ALU.add), [big, lnb], [dst])

        def proj_tm(n, hv, hbuf, wv, wbuf, c0, ncol, out_ps):
            for kc in range(8):
                PE(lambda h, kc=kc: h.matmul(out_ps[0:n, 0:ncol], lhsT=hv(kc)[:, 0:n], rhs=wv[:, kc, c0:c0 + ncol], start=(kc == 0), stop=(kc == 7)),
                   [hbuf, wbuf], [out_ps])

        def make_gt_bc(r, c0):
            for half in range(2):
                for c in range(4):
                    cc = half * 4 + c
                    V(lambda h, cc=cc: h.tensor_scalar(out=tmpc[:, 0:128], in0=ones_f[:], scalar1=adaT[:, c0 + cc, r:r + 1], scalar2=None, op0=ALU.mult),
                      [ones_f, adaT], [tmpc])
                    PE(lambda h, c=c: h.matmul(pg2[:, c * 128:(c + 1) * 128], lhsT=tmpc[:, 0:128], rhs=ident_f[:], start=True, stop=True),
                       [tmpc, ident_f], [pg2])
                V(lambda h, half=half: h.tensor_copy(out=gtbc[:, half * 512:(half + 1) * 512], in_=pg2[:]), [pg2], [gtbc])

        def make_gt_s(c0):
            for half in range(2):
                for c in range(4):
                    cc = half * 4 + c
                    V(lambda h, cc=cc: h.tensor_copy(out=tmpc[:, 0:SB], in_=adaT[:, c0 + cc, PB:NR]), [adaT], [tmpc])
                    PE(lambda h, c=c: h.transpose(out=pg2[0:SB, c * 128:(c + 1) * 128], in_=tmpc[:, 0:SB], identity=ident_f[:]),
                       [tmpc, ident_f], [pg2])
                V(lambda h, half=half: h.tensor_copy(out=gts[:, half * 512:(half + 1) * 512], in_=pg2[0:SB, :]), [pg2], [gts])

        def gla_logdecay(n, hv, hbuf):
            for kc in range(8):
                PE(lambda h, kc=kc: h.matmul(pg1[0:16, 0:n], lhsT=w1a[:, kc, 3072:3088], rhs=hv(kc)[:, 0:n], start=(kc == 0), stop=(kc == 7)),
                   [W1A, hbuf], [pg1])
            V(lambda h: h.tensor_copy(out=glT[:, 0:n], in_=pg1[0:16, 0:n]), [pg1], [glT])
            PE(lambda h: h.matmul(pg1[0:n, 0:256], lhsT=glT[:, 0:n], rhs=wg2[:], start=True, stop=True), [glT, wg2], [pg1])
            V(lambda h: h.tensor_tensor(out=pre[0:n, :], in0=pg1[0:n, 0:256], in1=bgbc[0:n, :], op=ALU.add), [pg1, bgbc], [pre])
            ACT(lambda h: h.activation(out=pre[0:n, :], in_=pre[0:n, :], func=AF.Exp, scale=-1.0), [pre], [pre])
            ACT(lambda h: h.activation(out=lpos[0:n, :], in_=pre[0:n, :], func=AF.Ln, bias=1.0), [pre], [lpos])

        def gla_outnorm(n, o_ps, sr, dst_b):
            ACT(lambda h: h.activation(out=og_f[0:n, :], in_=o_ps[0:n, :], func=AF.Square), [o_ps], [og_f])
            V(lambda h: h.tensor_reduce(out=ssq[0:n, :], in_=og_f[0:n, :].rearrange("p (h v) -> p h v", h=4), axis=AX.X, op=ALU.add), [og_f], [ssq])
            V(lambda h: h.tensor_scalar(out=ssq[0:n, :], in0=ssq[0:n, :], scalar1=1.0 / 128, scalar2=GN_EPS, op0=ALU.mult, op1=ALU.add), [ssq], [ssq])
            ACT(lambda h: h.activation(out=ssq[0:n, :], in_=ssq[0:n, :], func=AF.Ln), [ssq], [ssq])
            ACT(lambda h: h.activation(out=rinv[0:n, :], in_=ssq[0:n, :], func=AF.Exp, scale=-0.5), [ssq], [rinv])
            for hh in range(4):
                V(lambda h, hh=hh: h.scalar_tensor_tensor(out=og_f[0:n, hh * 128:(hh + 1) * 128], in0=o_ps[0:n, hh * 128:(hh + 1) * 128],
                                                         scalar=rinv[0:n, hh:hh + 1], in1=ngbc[0:n, :], op0=ALU.mult, op1=ALU.mult),
                  [o_ps, rinv, ngbc], [og_f])
            V(lambda h: h.tensor_tensor(out=dst_b[0:n, :], in0=og_f[0:n, :], in1=sr[0:n, :], op=ALU.mult), [og_f, sr], [dst_b])

        qselZ = sb([128, 2, 2, SB, SB], F32)
        V(lambda h: h.memset(qselZ[:].rearrange("p a b c d -> p (a b c d)"), 0.0), [], [qselZ])
        ptb = sb([128, NPG], I32)
        idx = sb([128, NPG], I32)
        kpg = [sb([128, 512], F32) for _ in range(3)]
        prod = [sb([128, 512], F32) for _ in range(1)]
        qbc = sb([128, 512], F32)
        zt = sb([128, PAGES, 8], F32)
        sps = sb([128, PAGES, 8], BF16)
        tots = sb([128, 8], F32)
        rhsE = sb([128, PAGES, 8], BF16)
        As = sb([128, PAGES, 8], F32)
        osb_tm_b = sb([SB, 512], BF16)
        qk_s = sb([SB, 512], F32)
        aT = sb([128, 2, SB], F32)
        kTs = sb([128, 2, SB], F32)
        qTs = sb([128, 2, SB], F32)
        qsel = sb([128, 2, SB, SB], F32)
        Ss = sb([128, 2, 128], F32)
        a_s = sb([SB, 256], F32)
        gv_s = sb([SB, 512], F32)
        LD(ptb[:], ptab.partition_broadcast(128), [ptb])
        V(lambda h: h.tensor_scalar(out=idx[:], in0=ptb[:], scalar1=128.0, scalar2=iota_p[:, 0:1], op0=ALU.mult, op1=ALU.add),
          [ptb, iota_p], [idx])
        osb_acc = sb([SB, 512], F32)
        V(lambda h: h.memset(osb_acc[:], 0.0), [], [osb_acc])

        def sample_flow():
            LD(xt[0][0:SB, :], x_s, [xt[0]])
            hb, hv = ln_to_hT(xt[0], SB, None, 0, 8)
            proj_tm(SB, hv, hb, w1a, W1A, 512, 512, pm[0])
            V(lambda h: h.tensor_copy(out=kst[0:SB, :], in_=pm[0][0:SB, :]), [pm[0]], [kst])
            ST(nk_s, kst[0:SB, :], [kst])
            proj_tm(SB, hv, hb, w1a, W1A, 1024, 512, pm[1])
            V(lambda h: h.tensor_copy(out=vst[0:SB, :], in_=pm[1][0:SB, :]), [pm[1]], [vst])
            ST(nv_s, vst[0:SB, :], [vst])
            proj_tm(SB, hv, hb, w1a, W1A, 2048, 512, pm[0])
            V(lambda h: h.tensor_copy(out=gv_s[:], in_=pm[0][0:SB, :]), [pm[0]], [gv_s])
            proj_tm(SB, hv, hb, w1a, W1A, 2560, 512, pm[0])
            ACT(lambda h: h.activation(out=srg[0:SB, :], in_=pm[0][0:SB, :], func=AF.Silu), [pm[0]], [srg])
            gla_logdecay(SB, hv, hb)
            ACT(lambda h: h.activation(out=a_s[:], in_=lpos[0:SB, :], func=AF.Exp, scale=-1.0 / 16), [lpos], [a_s])
            proj_tm(SB, hv, hb, w1a, W1A, 1536, 512, pm[1])
            V(lambda h: h.tensor_copy(out=qk_s[:], in_=pm[1][0:SB, :]), [pm[1]], [qk_s])
            for g in range(2):
                for j, src in enumerate((a_s[:, g * 128:(g + 1) * 128], qk_s[:, 256 + g * 128:256 + (g + 1) * 128], qk_s[:, g * 128:(g + 1) * 128])):
                    PE(lambda h, src=src, g=g, j=j: h.transpose(out=pg1[:, (j * 2 + g) * SB:(j * 2 + g + 1) * SB], in_=src, identity=ident_f[0:SB, 0:SB]),
                       [a_s, qk_s, ident_f], [pg1])
            V(lambda h: h.tensor_copy(out=aT[:], in_=pg1[:, 0:2 * SB].rearrange("p (g t) -> p g t", g=2)), [pg1], [aT])
            V(lambda h: h.tensor_copy(out=kTs[:], in_=pg1[:, 2 * SB:4 * SB].rearrange("p (g t) -> p g t", g=2)), [pg1], [kTs])
            V(lambda h: h.tensor_scalar(out=qTs[:], in0=pg1[:, 4 * SB:6 * SB].rearrange("p (g t) -> p g t", g=2), scalar1=0.125, scalar2=None, op0=ALU.mult), [pg1], [qTs])
            V(lambda h: h.tensor_tensor(out=qsel[:], in0=qTs[:].unsqueeze(3).to_broadcast([128, 2, SB, SB]),
                                        in1=eye4[:, 0:SB, 0:SB].unsqueeze(1).to_broadcast([128, 2, SB, SB]), op=ALU.mult), [qTs, eye4], [qsel])
            V(lambda h: h.tensor_copy(out=qselZ[0:64, 0, :, :, :], in_=qsel[0:64, :, :, :]), [qsel], [qselZ])
            V(lambda h: h.tensor_copy(out=qselZ[64:128, 1, :, :, :], in_=qsel[64:128, :, :, :]), [qsel], [qselZ])
            for j in range(SB):
                PE(lambda h, j=j: h.matmul(pz[1][:, :], lhsT=sel4[0:SB, j, :], rhs=gv_s[:], start=True, stop=True), [sel4, gv_s], [pz[1]])
                for g in range(2):
                    LD(Ss[:, g, :], state[j, 2 * g:2 * g + 2, :, :].rearrange("h k v -> (h k) v"), [Ss])
                for g in range(2):
                    V(lambda h, g=g, j=j: h.tensor_scalar(out=Ss[:, g, :], in0=Ss[:, g, :], scalar1=aT[:, g, j:j + 1], scalar2=None, op0=ALU.mult), [Ss, aT], [Ss])
                    for hh in range(2):
                        pr = slice(hh * 64, (hh + 1) * 64)
                        hd = 2 * g + hh
                        V(lambda h, g=g, j=j, pr=pr, hd=hd: h.scalar_tensor_tensor(out=Ss[pr, g, :], in0=pz[1][pr, hd * 128:(hd + 1) * 128], scalar=kTs[pr, g, j:j + 1],
                                                                                   in1=Ss[pr, g, :], op0=ALU.mult, op1=ALU.add), [pz[1], kTs, Ss], [Ss])
                    ST(ns_s[j, 2 * g:2 * g + 2, :, :].rearrange("h k v -> (h k) v"), Ss[:, g, :], [Ss])
                for hd in range(4):
                    g, pbase = hd // 2, (hd % 2) * 64
                    PE(lambda h, hd=hd, g=g, j=j: h.matmul(po[0:SB, hd * 128:(hd + 1) * 128], lhsT=qselZ[:, hd % 2, g, j, :], rhs=Ss[:, g, :],
                                                         start=(j == 0 and hd == 0), stop=(j == SB - 1)), [qselZ, Ss], [po])
            gla_outnorm(SB, po, srg, og_b)
            for hd in range(4):
                PE(lambda h, hd=hd: h.transpose(out=ptp[:, hd * SB:(hd + 1) * SB], in_=og_b[0:SB, hd * 128:(hd + 1) * 128], identity=ident_b[0:SB, 0:SB]), [og_b, ident_b], [ptp])
            V(lambda h: h.tensor_copy(out=ogT[:, :, 0:SB], in_=ptp[:, 0:4 * SB].rearrange("p (g t) -> p g t", g=4)), [ptp], [ogT])
            P.dma(sp, lambda h: h.dma_start(out=sc_og_s.rearrange("g p t -> p g t"), in_=ogT[:, :, 0:SB]), [ogT], [DB(d_og_s)])

            pgc = [0]
            for j in range(SB):
                V(lambda h, j=j: h.tensor_copy(out=hT[:, :, :], in_=hTs[:, :, j:j + 1].to_broadcast([128, 8, 128])), [hTs], [hT])
                proj_tm(128, lambda kc: hT[:, kc, :], hT, w1a, W1A, 0, 512, pm[0])
                V(lambda h: h.tensor_scalar(out=qbc[:], in0=pm[0][:], scalar1=0.125, scalar2=None, op0=ALU.mult), [pm[0]], [qbc])
                for pg in range(PAGES):
                    col = j * PAGES + pg
                    kp = kpg[pgc[0] % 3]
                    pd = prod[0]
                    pgc[0] += 1
                    P.dma(pool, lambda h, kp=kp, col=col: h.indirect_dma_start(out=kp[:], out_offset=None, in_=cache_k,
                                                                              in_offset=bass.IndirectOffsetOnAxis(ap=idx[:, col:col + 1], axis=0)), [idx], [kp])
                    V(lambda h, kp=kp, pd=pd: h.tensor_tensor(out=pd[:], in0=kp[:], in1=qbc[:], op=ALU.mult), [kp, qbc], [pd])
                    V(lambda h, pd=pd, pg=pg: h.tensor_reduce(out=zt[:, pg, :], in_=pd[:].rearrange("p (h d) -> p h d", h=8), axis=AX.X, op=ALU.add), [pd], [zt])
                    yield
                V(lambda h: h.tensor_tensor(out=zt[:], in0=zt[:], in1=biasbc[:].unsqueeze(1).to_broadcast([128, PAGES, 8]), op=ALU.add), [zt, biasbc], [zt])
                ACT(lambda h: h.activation(out=As[:], in_=zt[:], func=AF.Exp), [zt], [As])
                ACT(lambda h: h.activation(out=sps[:], in_=As[:], func=AF.Ln, bias=1.0), [As], [sps])
                for hd in range(8):
                    PE(lambda h, hd=hd: h.matmul(pg1[0:PAGES, hd:hd + 1], lhsT=sps[:, :, hd], rhs=negones[:, 0:1], start=True, stop=True), [sps, negones], [pg1])
                V(lambda h: h.tensor_copy(out=tots[0:PAGES, :], in_=pg1[0:PAGES, 0:8]), [pg1], [tots])
                V(lambda h: h.tensor_tensor(out=rhsE[0:PAGES, :, :], in0=tots[0:PAGES, :].unsqueeze(1).to_broadcast([PAGES, PAGES, 8]),
                                            in1=umat[0:PAGES, 0:PAGES].unsqueeze(2).to_broadcast([PAGES, PAGES, 8]), op=ALU.mult), [tots, umat], [rhsE])
                ncol = PAGES * 8
                spf = sps[:].rearrange("p a b -> p (a b)")
                ref = rhsE[:].rearrange("p a b -> p (a b)")
                ztf = zt[:].rearrange("p a b -> p (a b)")
                Asf = As[:].rearrange("p a b -> p (a b)")
                for c0 in range(0, ncol, 512):
                    cw = min(512, ncol - c0)
                    z = pz[(c0 // 512) % 2]
                    PE(lambda h, z=z, c0=c0, cw=cw: h.matmul(z[:, 0:cw], lhsT=negtri[:], rhs=spf[:, c0:c0 + cw], start=True, stop=False), [negtri, sps], [z])
                    PE(lambda h, z=z, c0=c0, cw=cw: h.matmul(z[:, 0:cw], lhsT=ones_b[0:PAGES, :], rhs=ref[0:PAGES, c0:c0 + cw], start=False, stop=True), [ones_b, rhsE], [z])
                    V(lambda h, z=z, c0=c0, cw=cw: h.tensor_tensor(out=ztf[:, c0:c0 + cw], in0=ztf[:, c0:c0 + cw], in1=z[:, 0:cw], op=ALU.add), [zt, z], [zt])
                ACT(lambda h: h.activation(out=As[:], in_=zt[:], func=AF.Exp), [zt], [As])
                for pg in range(PAGES):
                    col = j * PAGES + pg
                    kp = kpg[pgc[0] % 3]
                    pgc[0] += 1
                    P.dma(pool, lambda h, kp=kp, col=col: h.indirect_dma_start(out=kp[:], out_offset=None, in_=cache_v,
                                                                              in_offset=bass.IndirectOffsetOnAxis(ap=idx[:, col:col + 1], axis=0)), [idx], [kp])
                    PE(lambda h, kp=kp, pg=pg: h.matmul(pg2[0:8, :], lhsT=As[:, pg, :], rhs=kp[:], start=(pg == 0), stop=(pg == PAGES - 1)), [As, kp], [pg2])
                    yield
                V(lambda h: h.tensor_tensor(out=msk[:], in0=pg2[0:8, :], in1=bd8[:], op=ALU.mult), [pg2, bd8], [msk])
                PE(lambda h, j=j: h.matmul(pz[1][0:SB, :], lhsT=eye4[0:8, j, 0:SB], rhs=msk[:], start=True, stop=True), [eye4, msk], [pz[1]])
                V(lambda h: h.tensor_tensor(out=osb_acc[:], in0=osb_acc[:], in1=pz[1][0:SB, :], op=ALU.add), [osb_acc, pz[1]], [osb_acc])
            V(lambda h: h.tensor_copy(out=osb_tm_b[:], in_=osb_acc[:]), [osb_acc], [osb_tm_b])
            for hd in range(8):
                PE(lambda h, hd=hd: h.transpose(out=ptp[0:64, hd * SB:(hd + 1) * SB], in_=osb_tm_b[:, hd * 64:(hd + 1) * 64], identity=ident_b[0:SB, 0:SB]), [osb_tm_b, ident_b], [ptp])
            V(lambda h: h.tensor_copy(out=osb[:, :, 0:SB], in_=ptp[0:64, 0:8 * SB].rearrange("p (g t) -> p g t", g=8)), [ptp], [osb])
            P.dma(sp, lambda h: h.dma_start(out=sc_osb_s.rearrange("g p t -> p g t"), in_=osb[:, :, 0:SB]), [osb], [DB(d_osb_s)])


        stop_at(1)
        def attention_chunk(s, ch):
            tok0 = s * SEQ + ch * 512
            kmax = ch * 4 + 3
            for hd in range(8):
                cc, pbase = hd // 2, (hd % 2) * 64
                POOL(lambda h, hd=hd, cc=cc: h.memset(r_t[:], 0.0), [], [r_t])
                for i, kb in enumerate(range(kmax, -1, -1)):
                    c0 = max(0, (kb - ch * 4)) * 128
                    diag = kb >= ch * 4
                    z = pz[i % 2]
                    e_, sp_, a_ = e_t[i % 2], sp_t[i % 2], a_t[i % 2]
                    PE(lambda h, hd=hd, cc=cc, z=z, kb=kb, c0=c0, diag=diag: h.matmul(z[:, c0:512], lhsT=kT[:, cc, kb * 128:(kb + 1) * 128],
                                                                         rhs=qTz[:, hd % 2, cc, c0:512], start=True, stop=False),
                       [kT, qTz], [z])
                    if diag:
                        PE(lambda h, hd=hd, cc=cc, z=z, c0=c0: h.matmul(z[:, c0:c0 + 128], lhsT=ident_b[:], rhs=negm[:], start=False, stop=False),
                           [ident_b, negm], [z])
                    ACT(lambda h, hd=hd, cc=cc, z=z, e_=e_, c0=c0: h.activation(out=e_[:, c0:512], in_=z[:, c0:512], func=AF.Exp, bias=biasbc[:, hd:hd + 1]),
                        [z, biasbc], [e_])
                    ACT(lambda h, hd=hd, cc=cc, e_=e_, sp_=sp_, c0=c0: h.activation(out=sp_[:, c0:512], in_=e_[:, c0:512], func=AF.Ln, bias=1.0), [e_], [sp_])
                    PE(lambda h, hd=hd, cc=cc, z=z, sp_=sp_, c0=c0: h.matmul(z[:, c0:512], lhsT=negtri[:], rhs=sp_[:, c0:512], start=False, stop=False),
                       [negtri, sp_], [z])
                    PE(lambda h, hd=hd, cc=cc, z=z, c0=c0: h.matmul(z[:, c0:512], lhsT=negones[:], rhs=r_t[:, c0:512], start=False, stop=True),
                       [negones, r_t], [z])
                    ACT(lambda h, hd=hd, cc=cc, z=z, a_=a_, c0=c0: h.activation(out=a_[:, c0:512], in_=z[:, c0:512], func=AF.Exp, bias=biasbc[:, hd:hd + 1]),
                        [z, biasbc], [a_])
                    if kb > 0:
                        V(lambda h, hd=hd, cc=cc, sp_=sp_, c0=c0: h.tensor_tensor(out=r_t[:, c0:512], in0=r_t[:, c0:512], in1=sp_[:, c0:512], op=ALU.add),
                          [r_t, sp_], [r_t])
                    PE(lambda h, hd=hd, cc=cc, a_=a_, kb=kb, c0=c0, i=i: h.matmul(po[0:64, c0:512], lhsT=vseq[:, kb, hd * 64:(hd + 1) * 64], rhs=a_[:, c0:512],
                                                                  start=(i == 0), stop=(kb == 0)), [vseq, a_], [po])
                V(lambda h, hd=hd, cc=cc: h.tensor_copy(out=osT[:], in_=po[0:64, :]), [po], [osT])
                P.dma(sp, lambda h, hd=hd, cc=cc: h.dma_start(out=sc_osb[hd, :, tok0:tok0 + 512], in_=osT[:]), [osT], [DB(d_osb)])
                pump(8)

        def gla_tile(s, tl, tok0, sr):
            PE(lambda h: h.matmul(pg1[:, 0:256], lhsT=blktri[:], rhs=lpos[:], start=True, stop=True), [blktri, lpos], [pg1])
            ACT(lambda h: h.activation(out=eb[:], in_=pg1[:, 0:256], func=AF.Exp, scale=-1.0 / 16), [pg1], [eb])
            ACT(lambda h: h.activation(out=einv[:], in_=pg1[:, 0:256], func=AF.Exp, scale=1.0 / 16), [pg1], [einv])
            for g in range(2):
                PE(lambda h, g=g: h.matmul(pg1[:, 256 + 2 * g:258 + 2 * g], lhsT=lpos[:, g * 128:(g + 1) * 128], rhs=ind2[:], start=True, stop=True),
                   [lpos, ind2], [pg1])
            ACT(lambda h: h.activation(out=dec[:], in_=pg1[:, 256:260].rearrange("p (g c) -> p g c", g=2), func=AF.Exp, scale=-1.0 / 16), [pg1], [dec])
            stop_at(1.41)
            V(lambda h: h.scalar_tensor_tensor(out=qdec[:], in0=pm[1][:, 0:256], scalar=0.125, in1=eb[:], op0=ALU.mult, op1=ALU.mult), [pm[1], eb], [qdec])
            V(lambda h: h.tensor_tensor(out=kinv[:], in0=pm[1][:, 256:512], in1=einv[:], op=ALU.mult), [pm[1], einv], [kinv])
            for g in range(2):
                PE(lambda h, g=g: h.transpose(out=ptp[:, g * 128:(g + 1) * 128], in_=qdec[:, g * 128:(g + 1) * 128], identity=ident_b[:]), [qdec, ident_b], [ptp])
                PE(lambda h, g=g: h.transpose(out=ptp[:, 256 + g * 128:256 + (g + 1) * 128], in_=kinv[:, g * 128:(g + 1) * 128], identity=ident_b[:]), [kinv, ident_b], [ptp])
            V(lambda h: h.tensor_copy(out=qdT[:], in_=ptp[:, 0:256].rearrange("p (g t) -> p g t", g=2)), [ptp], [qdT])
            V(lambda h: h.tensor_copy(out=kiT[:], in_=ptp[:, 256:512].rearrange("p (g t) -> p g t", g=2)), [ptp], [kiT])
            stop_at(1.42)
            V(lambda h: h.tensor_copy(out=qdZ[0:64, 0, :, :], in_=qdT[0:64, :, :]), [qdT], [qdZ])
            V(lambda h: h.tensor_copy(out=qdZ[64:128, 1, :, :], in_=qdT[64:128, :, :]), [qdT], [qdZ])
            for hd in range(4):
                g, par = hd // 2, hd % 2
                PE(lambda h, hd=hd, g=g, par=par: h.matmul(pz[0][:, hd * 128:(hd + 1) * 128], lhsT=kiT[:, g, :], rhs=qdZ[:, par, g, :],
                                                         start=True, stop=True), [kiT, qdZ], [pz[0]])
            V(lambda h: h.tensor_tensor(out=attT[:].rearrange("p (h i) -> p h i", h=4), in0=pz[0][:].rearrange("p (h i) -> p h i", h=4),
                                        in1=blktri[:].unsqueeze(1).to_broadcast([128, 4, 128]), op=ALU.mult), [pz[0], blktri], [attT])
            stop_at(1.43)
            for hd in range(4):
                PE(lambda h, hd=hd: h.matmul(po[:, hd * 128:(hd + 1) * 128], lhsT=attT[:, hd * 128:(hd + 1) * 128], rhs=gvb[:, hd * 128:(hd + 1) * 128],
                                             start=(hd == 0), stop=False), [attT, gvb], [po])
            stop_at(1.44)
            for par in range(2):
                pr = slice(par * 64, (par + 1) * 64)
                for c in range(2):
                    V(lambda h, par=par, pr=pr, c=c: h.tensor_copy(out=qdZ2[pr, par, :, c, c * 64:(c + 1) * 64], in_=qdT[pr, :, c * 64:(c + 1) * 64]), [qdT], [qdZ2])
            for c in range(2):
                V(lambda h, c=c: h.tensor_copy(out=kinvZ[c * 64:(c + 1) * 64, c, :], in_=kinv[c * 64:(c + 1) * 64, :]), [kinv], [kinvZ])
            for c in range(2):
                for hd in range(4):
                    g, par = hd // 2, hd % 2
                    PE(lambda h, hd=hd, g=g, par=par, c=c: h.matmul(po[:, hd * 128:(hd + 1) * 128], lhsT=qdZ2[:, par, g, c, :],
                                                                  rhs=S16[:, g, :], start=False, stop=(c == 1)),
                       [qdZ2, S16], [po])
                for g in range(2):
                    PE(lambda h, g=g, c=c: h.matmul(pg1[:, g * 256:(g + 1) * 256], lhsT=kinvZ[:, c, g * 128:(g + 1) * 128], rhs=gvb[:, g * 256:(g + 1) * 256],
                                                   start=True, stop=True), [kinvZ, gvb], [pg1])
                kvv = pg1[:].rearrange("p (g x) -> p g x", g=2)
                V(lambda h, kvv=kvv: h.tensor_tensor(out=S32[0:64, :, :], in0=S32[0:64, :, :], in1=kvv[0:64, :, 0:128], op=ALU.add), [S32, pg1], [S32])
                V(lambda h, kvv=kvv: h.tensor_tensor(out=S32[64:128, :, :], in0=S32[64:128, :, :], in1=kvv[64:128, :, 128:256], op=ALU.add), [S32, pg1], [S32])
                V(lambda h, c=c: h.tensor_tensor(out=S32[:], in0=S32[:], in1=dec[:, :, c:c + 1].to_broadcast([128, 2, 128]), op=ALU.mult), [S32, dec], [S32])
                V(lambda h: h.tensor_copy(out=S16[:], in_=S32[:]), [S32], [S16])
            stop_at(1.45)
            gla_outnorm(128, po, sr, og_b)
            stop_at(1.46)
            for hd in range(4):
                PE(lambda h, hd=hd: h.transpose(out=ptp[:, hd * 128:(hd + 1) * 128], in_=og_b[:, hd * 128:(hd + 1) * 128], identity=ident_b[:]), [og_b, ident_b], [ptp])
            V(lambda h: h.tensor_copy(out=ogT[:], in_=ptp[:, 0:512].rearrange("p (g t) -> p g t", g=4)), [ptp], [ogT])
            P.dma(sp, lambda h: h.dma_start(out=sc_og[:, :, tok0:tok0 + 128].rearrange("g p t -> p g t"), in_=ogT[:]), [ogT], [DB(d_og)])

        tile_ctr = [0]
        sflow = sample_flow()
        sdone = [False]

        def pump(k):
            for _ in range(k):
                if sdone[0]:
                    return
                try:
                    next(sflow)
                except StopIteration:
                    sdone[0] = True

        for s in range(PB):
            V(lambda h: h.memset(S32[:], 0.0), [], [S32])
            V(lambda h: h.memset(S16[:], 0.0), [], [S16])
            for ch in range(NCH):
                for tl in range(4):
                    ti = ch * 4 + tl
                    tok0 = s * SEQ + ti * 128
                    xb = xt[tile_ctr[0] % 2]
                    tile_ctr[0] += 1
                    LD(xb[:], x_p[tok0:tok0 + 128, :], [xb])
                    hb, hv = ln_to_hT(xb, 128, s, 0, 8)
                    for grp in range(2):
                        for c4 in range(4):
                            cc = grp * 4 + c4
                            for kc in range(8):
                                PE(lambda h, kc=kc, cc=cc, c4=c4, grp=grp: h.matmul(pm[grp][:, c4 * 128:(c4 + 1) * 128], lhsT=w1a[:, kc, cc * 128:(cc + 1) * 128],
                                                                                 rhs=hT[:, kc, :], start=(kc == 0), stop=(kc == 7)), [W1A, hb], [pm[grp]])
                    ACT(lambda h, tl=tl: h.activation(out=qT[:, :, tl * 128:(tl + 1) * 128], in_=pm[0][:].rearrange("p (c t) -> p c t", c=4), func=AF.Copy, scale=0.125),
                        [pm[0]], [qT])
                    V(lambda h, ti=ti: h.tensor_copy(out=kT[:, :, ti * 128:(ti + 1) * 128], in_=pm[1][:].rearrange("p (c t) -> p c t", c=4)), [pm[1]], [kT])
                    V(lambda h, tl=tl: h.tensor_copy(out=qTz[0:64, 0, :, tl * 128:(tl + 1) * 128], in_=qT[0:64, :, tl * 128:(tl + 1) * 128]), [qT], [qTz])
                    V(lambda h, tl=tl: h.tensor_copy(out=qTz[64:128, 1, :, tl * 128:(tl + 1) * 128], in_=qT[64:128, :, tl * 128:(tl + 1) * 128]), [qT], [qTz])
                    proj_tm(128, hv, hb, w1a, W1A, 512, 512, pm[0])
                    V(lambda h: h.tensor_copy(out=kst[:], in_=pm[0][:]), [pm[0]], [kst])
                    stop_at(1.21)
                    ST(nk_p[tok0:tok0 + 128, :], kst[:], [kst])
                    stop_at(1.22)
                    proj_tm(128, hv, hb, w1a, W1A, 1024, 512, pm[1])
                    V(lambda h: h.tensor_copy(out=vst[:], in_=pm[1][:]), [pm[1]], [vst])
                    stop_at(1.23)
                    V(lambda h, ti=ti: h.tensor_copy(out=vseq[:, ti, :], in_=vst[:]), [vst], [vseq])
                    stop_at(1.24)
                    ST(nv_p[tok0:tok0 + 128, :], vst[:], [vst])
                    proj_tm(128, hv, hb, w1a, W1A, 2048, 512, pm[0])
                    V(lambda h: h.tensor_copy(out=gvb[:], in_=pm[0][:]), [pm[0]], [gvb])
                    proj_tm(128, hv, hb, w1a, W1A, 2560, 512, pm[0])
                    ACT(lambda h: h.activation(out=srg[:], in_=pm[0][:], func=AF.Silu), [pm[0]], [srg])
                    gla_logdecay(128, hv, hb)
                    proj_tm(128, hv, hb, w1a, W1A, 1536, 512, pm[1])
                    gla_tile(s, tl, tok0, srg)
                    pump(16)
                attention_chunk(s, ch)
            for g in range(2):
                ST(ns_p[s, 2 * g:2 * g + 2, :, :].rearrange("h k v -> (h k) v"), S32[:, g, :], [S32])

        stop_at(2)
        pump(10 ** 9)

        stop_at(4)
        P.barrier()
        off[0] = 0
        limit[0] = RAWN - TOPN
        LD(lng[:], ln1g.partition_broadcast(128), [lng])
        LD(lnb[:], ln1b.partition_broadcast(128), [lnb])
        wg_t = sb([128, 8, 2048], BF16)
        wua_t = sb([128, 8, 1024], BF16)
        wub_t = sb([128, 4, 1024], BF16)
        wo_t = sb([128, 8, 1024], BF16)
        wg, wua, wub, wo = wg_t.t, wua_t.t, wub_t.t, wo_t.t
        WG, WU = wg_t, wua_t
        gate = sb([128, 16, 128], BF16)
        t1 = sb([128, 512], F32)
        mT = sb([128, 8, 128], BF16)
        for kc in range(8):
            CASTLD(wg[:, kc, :], w_in_v[:, kc, N1A:NIN], [WG])
        w_upa_v = w_upa.rearrange("(h p) n -> p h n", p=64)
        for h_ in range(8):
            CASTLD(wua[0:64, h_, :], w_upa_v[:, h_, :], [WU])
        w_upb_v = w_upb.rearrange("(k p) n -> p k n", p=128)
        for kc in range(4):
            CASTLD(wub[:, kc, :], w_upb_v[:, kc, :], [WU])
        w_o_v = w_o.rearrange("(k p) n -> p k n", p=128)
        for kc in range(8):
            CASTLD(wo[:, kc, :], w_o_v[:, kc, :], [WU])
        def phase1b_front(n, xsrc_ap, r, osb_src, og_src, dbufs_in):
            par = tile_ctr[0] % 2
            xb = xt[par]
            tile_ctr[0] += 1
            LD(xb[0:n, :], xsrc_ap, [xb])
            hb, hv = ln_to_hT(xb, n, r, 0, 8, hT=(hT, hT2)[par])
            osb_, ogl_ = (osb, osb2)[par], (ogl, ogl2)[par]
            P.dma(sp, lambda h: h.dma_start(out=osb_[:, :, 0:n], in_=osb_src), [DB(dbufs_in[0])], [osb_])
            P.dma(sp, lambda h: h.dma_start(out=ogl_[:, :, 0:n], in_=og_src), [DB(dbufs_in[1])], [ogl_])
            return xb, hb, hv, osb_, ogl_

        def phase1b_back(ctx, n, gt, x1_dst_ap, dbuf_out):
            xb, hb, hv, osb, ogl = ctx
            for grp in range(4):
                z = pm[grp % 2]
                for c4 in range(4):
                    cc = grp * 4 + c4
                    for kc in range(8):
                        PE(lambda h, z=z, kc=kc, cc=cc, c4=c4: h.matmul(z[:, c4 * 128:c4 * 128 + n], lhsT=wg[:, kc, cc * 128:(cc + 1) * 128], rhs=hv(kc)[:, 0:n],
                                                                     start=(kc == 0), stop=(kc == 7)), [WG, hb], [z])
                ACT(lambda h, z=z, grp=grp: h.activation(out=gate[:, grp * 4:(grp + 1) * 4, 0:n], in_=z[:].rearrange("p (c t) -> p c t", c=4)[:, :, 0:n], func=AF.Sigmoid),
                    [z], [gate])
            for grp in range(2):
                for c4 in range(4):
                    cc = grp * 4 + c4
                    for hd in range(8):
                        PE(lambda h, hd=hd, cc=cc, c4=c4: h.matmul(pz[0][:, c4 * 128:c4 * 128 + n], lhsT=wua[0:64, hd, cc * 128:(cc + 1) * 128], rhs=osb[:, hd, 0:n],
                                                                start=(hd == 0), stop=(hd == 7)), [WU, osb], [pz[0]])
                    for kc in range(4):
                        PE(lambda h, kc=kc, cc=cc, c4=c4: h.matmul(pz[1][:, c4 * 128:c4 * 128 + n], lhsT=wub[:, kc, cc * 128:(cc + 1) * 128], rhs=ogl[:, kc, 0:n],
                                                                start=(kc == 0), stop=(kc == 3)), [WU, ogl], [pz[1]])
                t1v = t1[:].rearrange("p (c t) -> p c t", c=4)[:, :, 0:n]
                V(lambda h, grp=grp, t1v=t1v: h.tensor_tensor(out=t1v, in0=pz[0][:].rearrange("p (c t) -> p c t", c=4)[:, :, 0:n], in1=gate[:, grp * 4:(grp + 1) * 4, 0:n], op=ALU.mult),
                  [pz[0], gate], [t1])
                V(lambda h, grp=grp: h.tensor_tensor(out=mT[:, grp * 4:(grp + 1) * 4, 0:n], in0=pz[1][:].rearrange("p (c t) -> p c t", c=4)[:, :, 0:n],
                                                    in1=gate[:, 8 + grp * 4:8 + (grp + 1) * 4, 0:n], op=ALU.mult), [pz[1], gate], [mT])
                V(lambda h, grp=grp, t1v=t1v: h.tensor_tensor(out=mT[:, grp * 4:(grp + 1) * 4, 0:n], in0=mT[:, grp * 4:(grp + 1) * 4, 0:n], in1=t1v, op=ALU.add), [mT, t1], [mT])
            for half in range(2):
                for kc in range(8):
                    PE(lambda h, kc=kc, half=half: h.matmul(pm[half][0:n, :], lhsT=mT[:, kc, 0:n], rhs=wo[:, kc, half * 512:(half + 1) * 512], start=(kc == 0), stop=(kc == 7)),
                       [mT, WU], [pm[half]])
            residual_ln(n, xb, pm[0], pm[1], gt, big2)
            P.dma(sp, lambda h: h.dma_start(out=x1_dst_ap, in_=big2[0:n, :]), [big2], [DB(dbuf_out)])

        jobs = []
        for s in range(PB):
            for ti in range(NT_SEQ):
                tok0 = s * SEQ + ti * 128
                jobs.append((s if ti == 0 else None,
                             (128, x_p[tok0:tok0 + 128, :], s, sc_osb[:, :, tok0:tok0 + 128].rearrange("g p t -> p g t"),
                              sc_og[:, :, tok0:tok0 + 128].rearrange("g p t -> p g t"), (d_osb, d_og)),
                             (128, gtbc, sc_x1[tok0:tok0 + 128, :], d_x1)))
        jobs.append(("s", (SB, x_s, None, sc_osb_s.rearrange("g p t -> p g t"), sc_og_s.rearrange("g p t -> p g t"), (d_osb_s, d_og_s)),
                     (SB, gts, sc_x1_s, d_x1_s)))
        ctx = phase1b_front(*jobs[0][1])
        for i, (mk, fa, ba) in enumerate(jobs):
            nxt = phase1b_front(*jobs[i + 1][1]) if i + 1 < len(jobs) else None
            if mk == "s":
                make_gt_s(16)
            elif mk is not None:
                make_gt_bc(mk, 16)
            phase1b_back(ctx, *ba)
            ctx = nxt

        stop_at(5)
        P.barrier()
        off[0] = 0
        wf1_t = sb([128, 8, 4096], BF16)
        wf2_t = sb([128, 32, 1024], BF16)
        wf1, wf2 = wf1_t.t, wf2_t.t
        WF = wf1_t
        fT = sb([128, 32, 128], BF16)
        rl = sb([128, 512], BF16)
        w_ff1_v = w_ff1.rearrange("(k p) n -> p k n", p=128)
        w_ff2_v = w_ff2.rearrange("(k p) n -> p k n", p=128)
        for kc in range(8):
            for c in range(0, 4096, 2048):
                CASTLD(wf1[:, kc, c:c + 2048], w_ff1_v[:, kc, c:c + 2048], [WF])
        for kc in range(32):
            CASTLD(wf2[:, kc, :], w_ff2_v[:, kc, :], [WF])
        LD(lng[:], ln2g.partition_broadcast(128), [lng])
        LD(lnb[:], ln2b.partition_broadcast(128), [lnb])

        def phase2_front(n, xsrc_ap, dbuf_in, r):
            par = tile_ctr[0] % 2
            xb = xt[par]
            tile_ctr[0] += 1
            P.dma(sp, lambda h: h.dma_start(out=xb[0:n, :], in_=xsrc_ap), [DB(dbuf_in)], [xb])
            hb, hv = ln_to_hT(xb, n, r, 24, 32, hT=(hT, hT2)[par])
            return xb, hb, hv

        def phase2_back(ctx, n, gt, y_dst_ap):
            xb, hb, hv = ctx
            for grp in range(8):
                z = pz[grp % 2]
                for c4 in range(4):
                    cc = grp * 4 + c4
                    for kc in range(8):
                        PE(lambda h, z=z, kc=kc, cc=cc, c4=c4: h.matmul(z[:, c4 * 128:c4 * 128 + n], lhsT=wf1[:, kc, cc * 128:(cc + 1) * 128], rhs=hv(kc)[:, 0:n],
                                                                     start=(kc == 0), stop=(kc == 7)), [WF, hb], [z])
                rlv = rl[:].rearrange("p (c t) -> p c t", c=4)[:, :, 0:n]
                ACT(lambda h, z=z, rlv=rlv: h.activation(out=rlv, in_=z[:].rearrange("p (c t) -> p c t", c=4)[:, :, 0:n], func=AF.Relu), [z], [rl])
                V(lambda h, grp=grp, rlv=rlv: h.tensor_tensor(out=fT[:, grp * 4:(grp + 1) * 4, 0:n], in0=rlv, in1=rlv, op=ALU.mult), [rl], [fT])
            for half in range(2):
                for kc in range(32):
                    PE(lambda h, kc=kc, half=half: h.matmul(pm[half][0:n, :], lhsT=fT[:, kc, 0:n], rhs=wf2[:, kc, half * 512:(half + 1) * 512], start=(kc == 0), stop=(kc == 31)),
                       [fT, WF], [pm[half]])
            residual_ln(n, xb, pm[0], pm[1], gt, big2)
            ST(y_dst_ap, big2[0:n, :], [big2])

        jobs = []
        for s in range(PB):
            for ti in range(NT_SEQ):
                tok0 = s * SEQ + ti * 128
                jobs.append((s if ti == 0 else None, (128, sc_x1[tok0:tok0 + 128, :], d_x1, s), (128, gtbc, y_p[tok0:tok0 + 128, :])))
        jobs.append(("s", (SB, sc_x1_s, d_x1_s, None), (SB, gts, y_s)))
        ctx = phase2_front(*jobs[0][1])
        for i, (mk, fa, ba) in enumerate(jobs):
            nxt = phase2_front(*jobs[i + 1][1]) if i + 1 < len(jobs) else None
            if mk == "s":
                make_gt_s(40)
            elif mk is not None:
                make_gt_bc(mk, 40)
            phase2_back(ctx, *ba)
            ctx = nxt

    except StopBuild:
        pass

    sems = []
    import contextlib
    with contextlib.ExitStack() as es:
        for e in P.engs:
            e.sem = es.enter_context(nc.semaphore("s_" + e.name))
        for d_ in P.dsems:
            d_.sem = es.enter_context(nc.semaphore(d_.name))
        block = es.enter_context(nc.Block())
        P.emit(block)
    return nc


_CACHE = {}


def kernel(x_prompt, x_sample, cache_k, cache_v, state_gla, page_table, c_prompt, c_sample,
           w_ada, b_ada, w_in, sb_bias, w_gla_g2, b_gla_g, gla_norm_g, w_up_a, w_up_b, w_o,
           ln1_g, ln1_b, w_ff1, w_ff2, ln2_g, ln2_b, _cores=NCORES):
    f = lambda a: np.ascontiguousarray(np.asarray(a))
    B, SEQ, _ = x_prompt.shape
    DB_ = x_sample.shape[0]
    nphys = cache_k.shape[1]
    pages = page_table.shape[1]
    pb, sbn = B // _cores, DB_ // _cores
    cfg = Cfg(pb=pb, seq=SEQ, sb=sbn, pages=pages, nphys=nphys)
    key = (pb, SEQ, sbn, pages, nphys)
    if key not in _CACHE:
        _CACHE[key] = build(cfg)
    nc = _CACHE[key]
    ck = f(cache_k).reshape(nphys * 128, 512)
    cv = f(cache_v).reshape(nphys * 128, 512)
    shared = {
        "cache_k": ck, "cache_v": cv, "w_ada": f(w_ada[0]), "b_ada": f(b_ada[0]).reshape(1, -1), "w_in": f(w_in[0]),
        "sb_bias": f(sb_bias[0]).reshape(1, 8), "w_g2": f(w_gla_g2[0]), "b_g": f(b_gla_g[0]).reshape(1, 256),
        "normg": f(gla_norm_g[0]).reshape(1, 128), "w_upa": f(w_up_a[0]), "w_upb": f(w_up_b[0]), "w_o": f(w_o[0]),
        "ln1g": f(ln1_g[0]).reshape(1, D), "ln1b": f(ln1_b[0]).reshape(1, D), "w_ff1": f(w_ff1[0]), "w_ff2": f(w_ff2[0]),
        "ln2g": f(ln2_g[0]).reshape(1, D), "ln2b": f(ln2_b[0]).reshape(1, D),
    }
    in_maps = []
    for c in range(_cores):
        m = dict(shared)
        m["x_p"] = f(x_prompt[c * pb:(c + 1) * pb]).reshape(pb * SEQ, D)
        m["x_s"] = f(x_sample[c * sbn:(c + 1) * sbn]).reshape(sbn, D)
        m["state"] = f(state_gla[0, c * sbn:(c + 1) * sbn])
        m["ptab"] = f(page_table[c * sbn:(c + 1) * sbn]).reshape(1, sbn * pages).astype(np.int32)
        m["c_all"] = np.concatenate([f(c_prompt[c * pb:(c + 1) * pb]), f(c_sample[c * sbn:(c + 1) * sbn])], axis=0)
        in_maps.append(m)
    res = run_bass_kernel_spmd(nc, in_maps, core_ids=list(range(_cores)))
    R = res.results
    cat = lambda k: np.concatenate([np.asarray(r[k]) for r in R], axis=0)
    y_p = cat("y_p").reshape(B, SEQ, D)
    y_s = cat("y_s").reshape(DB_, 1, D)
    nk_p = cat("nk_p").reshape(1, B, SEQ, 8, 64)
    nv_p = cat("nv_p").reshape(1, B, SEQ, 8, 64)
    ns_p = cat("ns_p").reshape(1, B, 4, 64, 128)
    nk_s = cat("nk_s").reshape(1, DB_, 1, 8, 64)
    nv_s = cat("nv_s").reshape(1, DB_, 1, 8, 64)
    ns_s = cat("ns_s").reshape(1, DB_, 4, 64, 128)
    return tuple(np.ascontiguousarray(a, dtype=np.float32) for a in (y_p, y_s, nk_p, nv_p, ns_p, nk_s, nv_s, ns_s))
```

```python
import numpy as np
import concourse.bass as bass
import concourse.mybir as mybir
from concourse.bass_utils import run_bass_kernel_spmd

F32 = mybir.dt.float32
BF16 = mybir.dt.bfloat16
I32 = mybir.dt.int32
AF = mybir.ActivationFunctionType
ALU = mybir.AluOpType
AX = mybir.AxisListType

D = 1024
NIN = 5136
NCORES = 8
ALPHA = 2.0 ** 0.25
LN_EPS = 1e-5
GN_EPS = 1e-6
N1A = 3088


class StopBuild(Exception):
    pass


class Cfg:
    def __init__(self, pb=2, seq=2048, sb=4, pages=128, nphys=5120):
        self.pb, self.seq, self.sb, self.pages, self.nphys = pb, seq, sb, pages, nphys
        self.ntok = pb * seq


class Buf:
    __slots__ = ("w", "r", "ws")

    def __init__(self):
        self.w = None
        self.r = {}
        self.ws = []


class Eng:
    def __init__(self, name):
        self.name = name
        self.ops = []
        self.count = 0
        self.seen = {}
        self.sem = None


class DSem:
    def __init__(self, i):
        self.name = f"dma{i}"
        self.sem = None
        self.count = 0


class T:
    def __init__(self, t, buf=None):
        self.t = t
        self.b = buf or Buf()

    def __getitem__(self, k):
        return self.t[k]


class Prog:
    def __init__(self, nc):
        self.nc = nc
        self.pe, self.act, self.dve, self.pool, self.sp = (Eng(n) for n in ("pe", "act", "dve", "pool", "sp"))
        self.engs = [self.pe, self.act, self.dve, self.pool, self.sp]
        self.dsems = [DSem(i) for i in range(40)]
        self.dnext = 0
        self.out_toks = []

    def _waits(self, eng, reads, writes, extra=(), unchained=False):
        waits = {}

        def need(tok):
            if tok is None:
                return
            e, v = tok
            if waits.get(e, 0) < v:
                waits[e] = v
        for b in reads:
            need(b.b.w)
            for tok in b.b.ws:
                need(tok)
        for b in writes:
            need(b.b.w)
            if not unchained:
                for tok in b.b.ws:
                    need(tok)
            for e, v in b.b.r.items():
                need((e, v))
        for tok in extra:
            need(tok)
        wl = []
        for e, v in waits.items():
            if e is eng and eng is self.pe:
                continue
            if eng.seen.get(e, 0) < v:
                eng.seen[e] = v
                wl.append((e, v))
        return wl

    def op(self, eng, fn, reads=(), writes=()):
        wl = self._waits(eng, reads, writes)
        eng.count += 1
        tok = (eng, eng.count)
        eng.ops.append((wl, fn, eng, 1))
        for b in reads:
            b.b.r[eng] = eng.count
        for b in writes:
            b.b.w = tok
            b.b.ws = []
            b.b.r = {}
        return tok

    def dma(self, q, fn, reads=(), writes=(), is_out=False, unchained=False):
        ds = self.dsems[self.dnext % len(self.dsems)]
        self.dnext += 1
        extra = [(ds, ds.count)] if ds.count else []
        wl = self._waits(q, reads, writes, extra, unchained)
        ds.count += 16
        tok = (ds, ds.count)
        q.ops.append((wl, fn, ds, 16))
        for b in reads:
            b.b.r[ds] = ds.count
        for b in writes:
            if unchained:
                b.b.ws.append(tok)
            else:
                b.b.w = tok
                b.b.ws = []
                b.b.r = {}
        if is_out:
            self.out_toks.append(tok)
        return tok

    def barrier(self):
        targets = [(e, e.count) for e in self.engs if e.count > 0] + [(d, d.count) for d in self.dsems if d.count > 0]
        for eng in self.engs:
            wl = []
            for e, v in targets:
                if eng.seen.get(e, 0) < v:
                    eng.seen[e] = v
                    wl.append((e, v))
            eng.count += 1
            eng.ops.append((wl, lambda h: h.nop(), eng, 1))

    def emit(self, block):
        nc = self.nc
        fin = {}
        for e, v in self.out_toks:
            fin[e] = max(fin.get(e, 0), v)

        def run(eng, h):
            for wl, fn, semobj, inc in eng.ops:
                for e, v in wl:
                    h.wait_ge(e.sem, v)
                fn(h).then_inc(semobj.sem, inc)
            if eng is self.sp:
                for e, v in fin.items():
                    h.wait_ge(e.sem, v)

        block.sync(lambda h: run(self.sp, h))
        block.tensor(lambda h: run(self.pe, h))
        block.scalar(lambda h: run(self.act, h))
        block.vector(lambda h: run(self.dve, h))
        block.gpsimd(lambda h: run(self.pool, h))


def build(cfg):
    nc = bass.Bass("TRN2", target_bir_lowering=False)
    P = Prog(nc)
    pe, act, dve, pool, sp = P.pe, P.act, P.dve, P.pool, P.sp
    PB, SEQ, SB, PAGES, NPHYS, NTOK = cfg.pb, cfg.seq, cfg.sb, cfg.pages, cfg.nphys, cfg.ntok
    NR = PB + SB
    NT_SEQ = SEQ // 128
    NCH = SEQ // 512
    NPG = SB * PAGES

    def din(name, shape, dt=F32):
        return nc.dram_tensor(name, list(shape), dt, kind="ExternalInput").ap()

    def dout(name, shape, dt=F32):
        return nc.dram_tensor(name, list(shape), dt, kind="ExternalOutput").ap()

    x_p = din("x_p", [NTOK, D])
    x_s = din("x_s", [SB, D])
    cache_k = din("cache_k", [NPHYS * 128, 512])
    cache_v = din("cache_v", [NPHYS * 128, 512])
    state = din("state", [SB, 4, 64, 128])
    ptab = din("ptab", [1, NPG], I32)
    c_all = din("c_all", [NR, D])
    w_ada = din("w_ada", [D, 6 * D])
    b_ada = din("b_ada", [1, 6 * D])
    w_in = din("w_in", [D, NIN])
    sb_bias = din("sb_bias", [1, 8])
    w_g2 = din("w_g2", [16, 256])
    b_g = din("b_g", [1, 256])
    normg = din("normg", [1, 128])
    w_upa = din("w_upa", [512, D])
    w_upb = din("w_upb", [512, D])
    w_o = din("w_o", [D, D])
    ln1g = din("ln1g", [1, D])
    ln1b = din("ln1b", [1, D])
    w_ff1 = din("w_ff1", [D, 4 * D])
    w_ff2 = din("w_ff2", [4 * D, D])
    ln2g = din("ln2g", [1, D])
    ln2b = din("ln2b", [1, D])

    y_p = dout("y_p", [NTOK, D])
    y_s = dout("y_s", [SB, D])
    nk_p = dout("nk_p", [NTOK, 512])
    nv_p = dout("nv_p", [NTOK, 512])
    ns_p = dout("ns_p", [PB, 4, 64, 128])
    nk_s = dout("nk_s", [SB, 512])
    nv_s = dout("nv_s", [SB, 512])
    ns_s = dout("ns_s", [SB, 4, 64, 128])

    dbgk = {}
    sc_osb = nc.dram_tensor("sc_osb", [8, 64, NTOK], BF16, **dbgk).ap()
    sc_og = nc.dram_tensor("sc_og", [4, 128, NTOK], BF16, **dbgk).ap()
    sc_x1 = nc.dram_tensor("sc_x1", [NTOK, D], F32, **dbgk).ap()
    sc_osb_s = nc.dram_tensor("sc_osb_s", [8, 64, SB], BF16, **dbgk).ap()
    sc_og_s = nc.dram_tensor("sc_og_s", [4, 128, SB], BF16, **dbgk).ap()
    sc_x1_s = nc.dram_tensor("sc_x1_s", [SB, D], F32, **dbgk).ap()
    d_osb, d_og, d_x1, d_osb_s, d_og_s, d_x1_s = (Buf() for _ in range(6))

    class DB:
        def __init__(self, b):
            self.b = b

    _n = [0]

    def sbp(shape, dt=F32, name=None):
        _n[0] += 1
        return T(nc.alloc_sbuf_tensor(name or f"t{_n[0]}", list(shape), dt))

    RAWN = 82000
    TOPN = 8192
    rawh = []
    off = [0]

    limit = [RAWN]

    def sb(shape, dt=F32, name=None, at=None):
        shape = list(shape)
        n = int(np.prod(shape[1:]))
        ne = n * (2 if dt in (F32, I32) else 1)
        if at is None:
            a = (off[0] + 15) // 16 * 16
            off[0] = a + ne
            assert off[0] <= limit[0], (off[0], name)
        else:
            a = at
            assert a + ne <= RAWN
        v = rawh[0][0:shape[0], a:a + ne]
        if dt != BF16:
            v = v.bitcast(dt)
        if len(shape) == 3:
            v = v.rearrange("p (a b) -> p a b", a=shape[1])
        elif len(shape) == 4:
            v = v.rearrange("p (a b c) -> p a b c", a=shape[1], b=shape[2])
        elif len(shape) == 5:
            v = v.rearrange("p (a b c d) -> p a b c d", a=shape[1], b=shape[2], c=shape[3])
        return T(v)

    def ps(shape, dt=F32, name=None):
        _n[0] += 1
        return T(nc.alloc_psum_tensor(name or f"p{_n[0]}", list(shape), dt))

    ident_f = sbp([128, 128], F32)
    ident_b = sbp([128, 128], BF16)
    iota_p = sbp([128, 1], F32)
    iota_f = sbp([128, 512], F32)
    tri01 = sbp([128, 128], F32)
    negtri = sbp([128, 128], BF16)
    negones = sbp([128, 128], BF16)
    ones_b = sbp([128, 128], BF16)
    negm = sbp([128, 128], BF16)
    blktri = sbp([128, 128], F32)
    ind2 = sbp([128, 2], F32)
    umat = sbp([128, 128], F32)
    eye4 = sbp([128, 4, 4], F32)
    sel4 = sbp([4, 4, 128], F32)
    bd8 = sbp([8, 512], F32)
    msk = sbp([8, 512], F32)
    ones_f = sbp([128, 128], F32)
    tmpc = sbp([128, 512], F32)
    cpc = sbp([128, 1], F32)
    epsc = sbp([128, 1], F32)
    adaT = sbp([128, 48, NR], F32)
    bgbc = sbp([128, 256], F32)
    ngbc = sbp([128, 128], F32)
    biasbc = sbp([128, 8], F32)
    wg2 = sbp([16, 256], F32)
    xt = [sbp([128, D], F32) for _ in range(2)]
    xn = sbp([128, D], BF16)
    def stat_set():
        return (sbp([128, 12], F32), sbp([128, 2], F32), sbp([128, 1], F32), sbp([128, 1], F32), sbp([128, 1], F32))
    STA = stat_set()
    STB = stat_set()
    hT2 = sbp([128, 8, 128], BF16)
    osb2 = sbp([64, 8, 128], BF16)
    ogl2 = sbp([128, 4, 128], BF16)
    hT = sbp([128, 8, 128], BF16)
    hTs = sbp([128, 8, SB], BF16)
    tmp84 = sbp([128, 8, SB], F32)
    big = sbp([128, D], F32)
    big2 = sbp([128, D], F32)
    osb = sbp([64, 8, 128], BF16)
    ogl = sbp([128, 4, 128], BF16)
    ogT = sbp([128, 4, 128], BF16)

    rawh.append(nc.alloc_sbuf_tensor("raw", [128, RAWN], BF16))
    lng = sb([128, D], F32, at=RAWN - TOPN)
    lnb = sb([128, D], F32, at=RAWN - TOPN + 2048)
    gtbc = sb([128, D], F32, at=RAWN - TOPN + 4096)
    gts = sb([SB, D], F32, at=RAWN - TOPN + 6144)
    w1a_t = sb([128, 8, N1A], BF16)
    w1a = w1a_t.t
    W1A = w1a_t
    mark_1a = off[0]
    wada_t = [sb([128, 8, 512], BF16) for _ in range(2)]
    wada = [w.t for w in wada_t]
    WADA = wada_t
    cin = sb([NR, D], F32)
    csl = sb([NR, D], BF16)
    cT = sb([128, 8, NR], BF16)
    adag = sb([NR, 512], F32)
    badag = sb([NR, 512], F32)

    pm = [ps([128, 512], F32) for _ in range(2)]
    ptp = ps([128, 1024], BF16)
    pz = [ps([128, 512], F32) for _ in range(2)]
    po = ps([128, 512], F32)
    pg1 = ps([128, 512], F32)
    pg2 = ps([128, 512], F32)

    def V(fn, reads, writes):
        return P.op(dve, fn, reads, writes)

    def ACT(fn, reads, writes):
        return P.op(act, fn, reads, writes)

    def PE(fn, reads, writes):
        return P.op(pe, fn, reads, writes)

    def POOL(fn, reads, writes):
        return P.op(pool, fn, reads, writes)

    def LD(out_ap, in_ap, writes, reads=(), q=None):
        return P.dma(q or sp, lambda h: h.dma_start(out=out_ap, in_=in_ap), reads, writes)

    def ST(out_ap, in_ap, reads, writes=(), is_out=True):
        return P.dma(sp, lambda h: h.dma_start(out=out_ap, in_=in_ap), reads, writes, is_out=is_out)

    def CASTLD(out_ap, in_ap, writes):
        return P.dma(pool, lambda h: h.dma_start(out=out_ap, in_=in_ap), (), writes, unchained=True)

    STOP = 99.0

    def stop_at(k):
        if STOP <= k:
            raise StopBuild()

    try:
        POOL(lambda h: h.iota(iota_p[:], pattern=[[0, 1]], base=0, channel_multiplier=1,
                              allow_small_or_imprecise_dtypes=True), [], [iota_p])
        POOL(lambda h: h.iota(iota_f[:], pattern=[[1, 512]], base=0, channel_multiplier=0,
                              allow_small_or_imprecise_dtypes=True), [], [iota_f])
        POOL(lambda h: h.iota(eye4[:], pattern=[[1, 4], [-1, 4]], base=0, channel_multiplier=0,
                              allow_small_or_imprecise_dtypes=True), [], [eye4])
        POOL(lambda h: h.iota(sel4[:], pattern=[[1, 4], [0, 128]], base=0, channel_multiplier=-1,
                              allow_small_or_imprecise_dtypes=True), [], [sel4])
        POOL(lambda h: h.iota(bd8[:], pattern=[[1, 512]], base=0, channel_multiplier=-64,
                              allow_small_or_imprecise_dtypes=True), [], [bd8])
        V(lambda h: h.memset(ones_f[:], 1.0), [], [ones_f])
        V(lambda h: h.memset(negones[:], -1.0), [], [negones])
        V(lambda h: h.memset(ones_b[:], 1.0), [], [ones_b])
        V(lambda h: h.memset(epsc[:], LN_EPS), [], [epsc])
        V(lambda h: h.tensor_scalar(out=eye4[:], in0=eye4[:], scalar1=0.0, scalar2=None, op0=ALU.is_equal), [eye4], [eye4])
        V(lambda h: h.tensor_scalar(out=sel4[:], in0=sel4[:], scalar1=0.0, scalar2=None, op0=ALU.is_equal), [sel4], [sel4])
        V(lambda h: h.tensor_scalar(out=msk[:], in0=bd8[:], scalar1=0.0, scalar2=None, op0=ALU.is_ge), [bd8], [msk])
        V(lambda h: h.tensor_scalar(out=bd8[:], in0=bd8[:], scalar1=64.0, scalar2=None, op0=ALU.is_lt), [bd8], [bd8])
        V(lambda h: h.tensor_tensor(out=bd8[:], in0=bd8[:], in1=msk[:], op=ALU.mult), [bd8, msk], [bd8])
        F128 = iota_f[:, 0:128]
        V(lambda h: h.tensor_scalar(out=ident_f[:], in0=F128, scalar1=iota_p[:, 0:1], scalar2=None, op0=ALU.is_equal),
          [iota_f, iota_p], [ident_f])
        V(lambda h: h.tensor_copy(out=ident_b[:], in_=ident_f[:]), [ident_f], [ident_b])
        V(lambda h: h.tensor_scalar(out=tri01[:], in0=F128, scalar1=iota_p[:, 0:1], scalar2=None, op0=ALU.is_le),
          [iota_f, iota_p], [tri01])
        V(lambda h: h.tensor_scalar(out=negtri[:], in0=tri01[:], scalar1=-1.0, scalar2=None, op0=ALU.mult), [tri01], [negtri])
        V(lambda h: h.tensor_scalar(out=negm[:], in0=tri01[:], scalar1=-30000.0, scalar2=None, op0=ALU.mult), [tri01], [negm])
        V(lambda h: h.tensor_scalar(out=umat[:], in0=F128, scalar1=iota_p[:, 0:1], scalar2=None, op0=ALU.is_lt),
          [iota_f, iota_p], [umat])
        V(lambda h: h.tensor_scalar(out=cpc[:], in0=iota_p[:], scalar1=64.0, scalar2=None, op0=ALU.is_ge), [iota_p], [cpc])
        V(lambda h: h.tensor_scalar(out=tmpc[:, 0:128], in0=F128, scalar1=64.0, scalar2=None, op0=ALU.is_ge), [iota_f], [tmpc])
        V(lambda h: h.tensor_scalar(out=tmpc[:, 0:128], in0=tmpc[:, 0:128], scalar1=cpc[:, 0:1], scalar2=None, op0=ALU.is_equal),
          [tmpc, cpc], [tmpc])
        V(lambda h: h.tensor_scalar(out=blktri[:], in0=F128, scalar1=iota_p[:, 0:1], scalar2=None, op0=ALU.is_ge),
          [iota_f, iota_p], [blktri])
        V(lambda h: h.tensor_tensor(out=blktri[:], in0=blktri[:], in1=tmpc[:, 0:128], op=ALU.mult), [blktri, tmpc], [blktri])
        V(lambda h: h.tensor_copy(out=ind2[:, 1:2], in_=cpc[:]), [cpc], [ind2])
        V(lambda h: h.tensor_scalar(out=ind2[:, 0:1], in0=cpc[:], scalar1=-1.0, scalar2=1.0, op0=ALU.mult, op1=ALU.add), [cpc], [ind2])

        LD(bgbc[:], b_g.partition_broadcast(128), [bgbc])
        LD(ngbc[:], normg.partition_broadcast(128), [ngbc])
        LD(biasbc[:], sb_bias.partition_broadcast(128), [biasbc])
        LD(wg2[:], w_g2, [wg2])

        LD(cin[:], c_all, [cin])
        ACT(lambda h: h.activation(out=csl[:], in_=cin[:], func=AF.Silu), [cin], [csl])
        for c in range(8):
            PE(lambda h, c=c: h.transpose(out=ptp[:, c * NR:(c + 1) * NR], in_=csl[:, c * 128:(c + 1) * 128], identity=ident_b[0:NR, 0:NR]),
               [csl, ident_b], [ptp])
        V(lambda h: h.tensor_copy(out=cT[:], in_=ptp[:, 0:8 * NR].rearrange("p (c r) -> p c r", c=8)), [ptp], [cT])
        w_ada_v = w_ada.rearrange("(k p) n -> p k n", p=128)
        for g in range(12):
            wb = g % 2
            CASTLD(wada[wb], w_ada_v[:, :, g * 512:(g + 1) * 512], [WADA[wb]])
            LD(badag[:], b_ada[:, g * 512:(g + 1) * 512].partition_broadcast(NR), [badag])
            for kc in range(8):
                PE(lambda h, kc=kc, wb=wb: h.matmul(pm[0][0:NR, :], lhsT=cT[:, kc, :], rhs=wada[wb][:, kc, :], start=(kc == 0), stop=(kc == 7)),
                   [cT, WADA[wb]], [pm[0]])
            V(lambda h: h.tensor_tensor(out=adag[:], in0=pm[0][0:NR, :], in1=badag[:], op=ALU.add), [pm[0], badag], [adag])
            for j in range(4):
                PE(lambda h, j=j: h.transpose(out=pg1[:, j * NR:(j + 1) * NR], in_=adag[:, j * 128:(j + 1) * 128], identity=ident_f[0:NR, 0:NR]),
                   [adag, ident_f], [pg1])
            V(lambda h, g=g: h.tensor_copy(out=adaT[:, 4 * g:4 * g + 4, :], in_=pg1[:, 0:4 * NR].rearrange("p (c r) -> p c r", c=4)),
              [pg1], [adaT])
        V(lambda h: h.tensor_scalar(out=adaT[:, 8:16, :], in0=adaT[:, 8:16, :], scalar1=1.0, scalar2=None, op0=ALU.add), [adaT], [adaT])
        V(lambda h: h.tensor_scalar(out=adaT[:, 32:40, :], in0=adaT[:, 32:40, :], scalar1=1.0, scalar2=None, op0=ALU.add), [adaT], [adaT])

        stop_at(0)
        w_in_v = w_in.rearrange("(k p) n -> p k n", p=128)
        for kc in range(8):
            for (a, b_) in ((0, 1544), (1544, N1A)):
                CASTLD(w1a[:, kc, a:b_], w_in_v[:, kc, a:b_], [W1A])
        P.barrier()
        off[0] = mark_1a
        kst = sb([128, 512], F32)
        vst = sb([128, 512], F32)
        glT = sb([16, 128], F32)
        pre = sb([128, 256], F32)
        lpos = sb([128, 256], F32)
        srg = sb([128, 512], F32)
        og_f = sb([128, 512], F32)
        og_b = sb([128, 512], BF16)
        ssq = sb([128, 4], F32)
        rinv = sb([128, 4], F32)
        mark_shared = off[0]
        qT = sb([128, 4, 512], BF16)
        kT = sb([128, 4, SEQ], BF16)
        vseq = sb([128, NT_SEQ, 512], BF16)
        e_t = [sb([128, 512], BF16) for _ in range(2)]
        sp_t = [sb([128, 512], BF16) for _ in range(2)]
        a_t = [sb([128, 512], BF16) for _ in range(2)]
        r_t = sb([128, 512], BF16)
        osT = sb([64, 512], BF16)
        eb = sb([128, 256], F32)
        einv = sb([128, 256], F32)
        qdec = sb([128, 256], BF16)
        kinv = sb([128, 256], BF16)
        qdT = sb([128, 2, 128], BF16)
        qdZ = sb([128, 2, 2, 128], BF16)
        V(lambda h: h.memset(qdZ[:].rearrange("p a b c -> p (a b c)"), 0.0), [], [qdZ])
        qdZ2 = sb([128, 2, 2, 2, 128], BF16)
        V(lambda h: h.memset(qdZ2[:].rearrange("p a b c d -> p (a b c d)"), 0.0), [], [qdZ2])
        kinvZ = sb([128, 2, 256], BF16)
        V(lambda h: h.memset(kinvZ[:].rearrange("p a b -> p (a b)"), 0.0), [], [kinvZ])
        qTz = sb([128, 2, 4, 512], BF16)
        V(lambda h: h.memset(qTz[:].rearrange("p a b c -> p (a b c)"), 0.0), [], [qTz])
        kiT = sb([128, 2, 128], BF16)
        attT = sb([128, 512], BF16)
        gvb = sb([128, 512], BF16)
        S32 = sb([128, 2, 128], F32)
        S16 = sb([128, 2, 128], BF16)
        dec = sb([128, 2, 2], F32)

        def ln_stats(src, n, S=None):
            st6, mv, lnv, rstd, nmr = S or STA
            V(lambda h: h.bn_stats(out=st6[0:n, 0:6], in_=src[0:n, 0:512]), [src], [st6])
            V(lambda h: h.bn_stats(out=st6[0:n, 6:12], in_=src[0:n, 512:1024]), [src], [st6])
            V(lambda h: h.bn_aggr(out=mv[0:n, :], in_=st6[0:n, :]), [st6], [mv])
            ACT(lambda h: h.activation(out=lnv[0:n, :], in_=mv[0:n, 1:2], func=AF.Ln, bias=epsc[0:n, :]), [mv, epsc], [lnv])
            ACT(lambda h: h.activation(out=rstd[0:n, :], in_=lnv[0:n, :], func=AF.Exp, scale=-0.5), [lnv], [rstd])
            V(lambda h: h.scalar_tensor_tensor(out=nmr[0:n, :], in0=mv[0:n, 0:1], scalar=-1.0, in1=rstd[0:n, :], op0=ALU.mult, op1=ALU.mult),
              [mv, rstd], [nmr])
            return rstd, nmr

        def ln_to_hT(src, n, r, sh_c, sc_c, hT=hT):
            rstd, nmr = ln_stats(src, n)
            V(lambda h: h.tensor_scalar(out=xn[0:n, :], in0=src[0:n, :], scalar1=rstd[0:n, :], scalar2=nmr[0:n, :], op0=ALU.mult, op1=ALU.add),
              [src, rstd, nmr], [xn])
            for c in range(8):
                PE(lambda h, c=c: h.transpose(out=ptp[:, c * 128:c * 128 + n], in_=xn[0:n, c * 128:(c + 1) * 128], identity=ident_b[0:n, 0:n]),
                   [xn, ident_b], [ptp])
            if r is not None:
                for c in range(8):
                    V(lambda h, c=c: h.tensor_scalar(out=hT[:, c, :], in0=ptp[:, c * 128:(c + 1) * 128],
                                                     scalar1=adaT[:, sc_c + c, r:r + 1], scalar2=adaT[:, sh_c + c, r:r + 1],
                                                     op0=ALU.mult, op1=ALU.add), [ptp, adaT], [hT])
                return hT, lambda kc: hT[:, kc, :]
            tpv = ptp[:, :].rearrange("p (c t) -> p c t", c=8)[:, :, 0:n]
            V(lambda h: h.tensor_tensor(out=tmp84[:], in0=tpv, in1=adaT[:, sc_c:sc_c + 8, PB:NR], op=ALU.mult), [ptp, adaT], [tmp84])
            V(lambda h: h.tensor_tensor(out=hTs[:], in0=tmp84[:], in1=adaT[:, sh_c:sh_c + 8, PB:NR], op=ALU.add), [tmp84, adaT], [hTs])
            return hTs, lambda kc: hTs[:, kc, :]

        def residual_ln(n, xres, pa, pb_, gt, dst, dst_reads_extra=()):
            V(lambda h: h.tensor_tensor(out=big[0:n, 0:512], in0=pa[0:n, :], in1=gt[0:n, 0:512], op=ALU.mult), [pa, gt], [big])
            V(lambda h: h.tensor_tensor(out=big[0:n, 512:1024], in0=pb_[0:n, :], in1=gt[0:n, 512:1024], op=ALU.mult), [pb_, gt], [big])
            V(lambda h: h.scalar_tensor_tensor(out=big[0:n, :], in0=xres[0:n, :], scalar=ALPHA, in1=big[0:n, :], op0=ALU.mult, op1=ALU.add),
              [xres, big], [big])
            rstd, nmr = ln_stats(big, n, STB)
            V(lambda h: h.tensor_scalar(out=big[0:n, :], in0=big[0:n, :], scalar1=rstd[0:n, :], scalar2=nmr[0:n, :], op0=ALU.mult, op1=ALU.add),
              [big, rstd, nmr], [big])
            V(lambda h: h.tensor_tensor(out=big[0:n, :], in0=big[0:n, :], in1=lng[0:n, :], op=ALU.mult), [big, lng], [big])
            V(lambda h: h.tensor_tensor(out=dst[0:n, :], in0=big[0:n, :], in1=lnb[0:n, :], op=ALU.add), [big, lnb], [dst])

        def proj_tm(n, hv, hbuf, wv, wbuf, c0, ncol, out_ps):
            for kc in range(8):
                PE(lambda h, kc=kc: h.matmul(out_ps[0:n, 0:ncol], lhsT=hv(kc)[:, 0:n], rhs=wv[:, kc, c0:c0 + ncol], start=(kc == 0), stop=(kc == 7)),
                   [hbuf, wbuf], [out_ps])

        def make_gt_bc(r, c0):
            for half in range(2):
                for c in range(4):
                    cc = half * 4 + c
                    V(lambda h, cc=cc: h.tensor_scalar(out=tmpc[:, 0:128], in0=ones_f[:], scalar1=adaT[:, c0 + cc, r:r + 1], scalar2=None, op0=ALU.mult),
                      [ones_f, adaT], [tmpc])
                    PE(lambda h, c=c: h.matmul(pg2[:, c * 128:(c + 1) * 128], lhsT=tmpc[:, 0:128], rhs=ident_f[:], start=True, stop=True),
                       [tmpc, ident_f], [pg2])
                V(lambda h, half=half: h.tensor_copy(out=gtbc[:, half * 512:(half + 1) * 512], in_=pg2[:]), [pg2], [gtbc])

        def make_gt_s(c0):
            for half in range(2):
                for c in range(4):
                    cc = half * 4 + c
                    V(lambda h, cc=cc: h.tensor_copy(out=tmpc[:, 0:SB], in_=adaT[:, c0 + cc, PB:NR]), [adaT], [tmpc])
                    PE(lambda h, c=c: h.transpose(out=pg2[0:SB, c * 128:(c + 1) * 128], in_=tmpc[:, 0:SB], identity=ident_f[:]),
                       [tmpc, ident_f], [pg2])
                V(lambda h, half=half: h.tensor_copy(out=gts[:, half * 512:(half + 1) * 512], in_=pg2[0:SB, :]), [pg2], [gts])

        def gla_logdecay(n, hv, hbuf):
            for kc in range(8):
                PE(lambda h, kc=kc: h.matmul(pg1[0:16, 0:n], lhsT=w1a[:, kc, 3072:3088], rhs=hv(kc)[:, 0:n], start=(kc == 0), stop=(kc == 7)),
                   [W1A, hbuf], [pg1])
            V(lambda h: h.tensor_copy(out=glT[:, 0:n], in_=pg1[0:16, 0:n]), [pg1], [glT])
            PE(lambda h: h.matmul(pg1[0:n, 0:256], lhsT=glT[:, 0:n], rhs=wg2[:], start=True, stop=True), [glT, wg2], [pg1])
            V(lambda h: h.tensor_tensor(out=pre[0:n, :], in0=pg1[0:n, 0:256], in1=bgbc[0:n, :], op=ALU.add), [pg1, bgbc], [pre])
            ACT(lambda h: h.activation(out=pre[0:n, :], in_=pre[0:n, :], func=AF.Exp, scale=-1.0), [pre], [pre])
            ACT(lambda h: h.activation(out=lpos[0:n, :], in_=pre[0:n, :], func=AF.Ln, bias=1.0), [pre], [lpos])

        def gla_outnorm(n, o_ps, sr, dst_b):
            ACT(lambda h: h.activation(out=og_f[0:n, :], in_=o_ps[0:n, :], func=AF.Square), [o_ps], [og_f])
            V(lambda h: h.tensor_reduce(out=ssq[0:n, :], in_=og_f[0:n, :].rearrange("p (h v) -> p h v", h=4), axis=AX.X, op=ALU.add), [og_f], [ssq])
            V(lambda h: h.tensor_scalar(out=ssq[0:n, :], in0=ssq[0:n, :], scalar1=1.0 / 128, scalar2=GN_EPS, op0=ALU.mult, op1=ALU.add), [ssq], [ssq])
            ACT(lambda h: h.activation(out=ssq[0:n, :], in_=ssq[0:n, :], func=AF.Ln), [ssq], [ssq])
            ACT(lambda h: h.activation(out=rinv[0:n, :], in_=ssq[0:n, :], func=AF.Exp, scale=-0.5), [ssq], [rinv])
            for hh in range(4):
                V(lambda h, hh=hh: h.scalar_tensor_tensor(out=og_f[0:n, hh * 128:(hh + 1) * 128], in0=o_ps[0:n, hh * 128:(hh + 1) * 128],
                                                         scalar=rinv[0:n, hh:hh + 1], in1=ngbc[0:n, :], op0=ALU.mult, op1=ALU.mult),
                  [o_ps, rinv, ngbc], [og_f])
            V(lambda h: h.tensor_tensor(out=dst_b[0:n, :], in0=og_f[0:n, :], in1=sr[0:n, :], op=ALU.mult), [og_f, sr], [dst_b])

        qselZ = sb([128, 2, 2, SB, SB], F32)
        V(lambda h: h.memset(qselZ[:].rearrange("p a b c d -> p (a b c d)"), 0.0), [], [qselZ])
        ptb = sb([128, NPG], I32)
        idx = sb([128, NPG], I32)
        kpg = [sb([128, 512], F32) for _ in range(3)] + [T(big[:, 0:512]), T(big[:, 512:1024]), T(big2[:, 0:512]), T(big2[:, 512:1024])]
        prod = [sb([128, 512], F32) for _ in range(1)]
        qbc = sb([128, 512], F32)
        zt = sb([128, PAGES, 8], F32)
        sps = sb([128, PAGES, 8], BF16)
        tots = sb([128, 8], F32)
        rhsE = sb([128, PAGES, 8], BF16)
        As = sb([128, PAGES, 8], F32)
        osb_tm_b = sb([SB, 512], BF16)
        qk_s = sb([SB, 512], F32)
        aT = sb([128, 2, SB], F32)
        kTs = sb([128, 2, SB], F32)
        qTs = sb([128, 2, SB], F32)
        qsel = sb([128, 2, SB, SB], F32)
        Ss = sb([128, 2, 128], F32)
        a_s = sb([SB, 256], F32)
        gv_s = sb([SB, 512], F32)
        LD(ptb[:], ptab.partition_broadcast(128), [ptb])
        V(lambda h: h.tensor_scalar(out=idx[:], in0=ptb[:], scalar1=128.0, scalar2=iota_p[:, 0:1], op0=ALU.mult, op1=ALU.add),
          [ptb, iota_p], [idx])
        osb_acc = sb([SB, 512], F32)
        V(lambda h: h.memset(osb_acc[:], 0.0), [], [osb_acc])

        def sample_flow():
            LD(xt[0][0:SB, :], x_s, [xt[0]])
            hb, hv = ln_to_hT(xt[0], SB, None, 0, 8)
            proj_tm(SB, hv, hb, w1a, W1A, 512, 512, pm[0])
            V(lambda h: h.tensor_copy(out=kst[0:SB, :], in_=pm[0][0:SB, :]), [pm[0]], [kst])
            ST(nk_s, kst[0:SB, :], [kst])
            proj_tm(SB, hv, hb, w1a, W1A, 1024, 512, pm[1])
            V(lambda h: h.tensor_copy(out=vst[0:SB, :], in_=pm[1][0:SB, :]), [pm[1]], [vst])
            ST(nv_s, vst[0:SB, :], [vst])
            proj_tm(SB, hv, hb, w1a, W1A, 2048, 512, pm[0])
            V(lambda h: h.tensor_copy(out=gv_s[:], in_=pm[0][0:SB, :]), [pm[0]], [gv_s])
            proj_tm(SB, hv, hb, w1a, W1A, 2560, 512, pm[0])
            ACT(lambda h: h.activation(out=srg[0:SB, :], in_=pm[0][0:SB, :], func=AF.Silu), [pm[0]], [srg])
            gla_logdecay(SB, hv, hb)
            ACT(lambda h: h.activation(out=a_s[:], in_=lpos[0:SB, :], func=AF.Exp, scale=-1.0 / 16), [lpos], [a_s])
            proj_tm(SB, hv, hb, w1a, W1A, 1536, 512, pm[1])
            V(lambda h: h.tensor_copy(out=qk_s[:], in_=pm[1][0:SB, :]), [pm[1]], [qk_s])
            for g in range(2):
                for j, src in enumerate((a_s[:, g * 128:(g + 1) * 128], qk_s[:, 256 + g * 128:256 + (g + 1) * 128], qk_s[:, g * 128:(g + 1) * 128])):
                    PE(lambda h, src=src, g=g, j=j: h.transpose(out=pg1[:, (j * 2 + g) * SB:(j * 2 + g + 1) * SB], in_=src, identity=ident_f[0:SB, 0:SB]),
                       [a_s, qk_s, ident_f], [pg1])
            V(lambda h: h.tensor_copy(out=aT[:], in_=pg1[:, 0:2 * SB].rearrange("p (g t) -> p g t", g=2)), [pg1], [aT])
            V(lambda h: h.tensor_copy(out=kTs[:], in_=pg1[:, 2 * SB:4 * SB].rearrange("p (g t) -> p g t", g=2)), [pg1], [kTs])
            V(lambda h: h.tensor_scalar(out=qTs[:], in0=pg1[:, 4 * SB:6 * SB].rearrange("p (g t) -> p g t", g=2), scalar1=0.125, scalar2=None, op0=ALU.mult), [pg1], [qTs])
            V(lambda h: h.tensor_tensor(out=qsel[:], in0=qTs[:].unsqueeze(3).to_broadcast([128, 2, SB, SB]),
                                        in1=eye4[:, 0:SB, 0:SB].unsqueeze(1).to_broadcast([128, 2, SB, SB]), op=ALU.mult), [qTs, eye4], [qsel])
            V(lambda h: h.tensor_copy(out=qselZ[0:64, 0, :, :, :], in_=qsel[0:64, :, :, :]), [qsel], [qselZ])
            V(lambda h: h.tensor_copy(out=qselZ[64:128, 1, :, :, :], in_=qsel[64:128, :, :, :]), [qsel], [qselZ])
            for j in range(SB):
                PE(lambda h, j=j: h.matmul(pz[1][:, :], lhsT=sel4[0:SB, j, :], rhs=gv_s[:], start=True, stop=True), [sel4, gv_s], [pz[1]])
                for g in range(2):
                    LD(Ss[:, g, :], state[j, 2 * g:2 * g + 2, :, :].rearrange("h k v -> (h k) v"), [Ss])
                for g in range(2):
                    V(lambda h, g=g, j=j: h.tensor_scalar(out=Ss[:, g, :], in0=Ss[:, g, :], scalar1=aT[:, g, j:j + 1], scalar2=None, op0=ALU.mult), [Ss, aT], [Ss])
                    for hh in range(2):
                        pr = slice(hh * 64, (hh + 1) * 64)
                        hd = 2 * g + hh
                        V(lambda h, g=g, j=j, pr=pr, hd=hd: h.scalar_tensor_tensor(out=Ss[pr, g, :], in0=pz[1][pr, hd * 128:(hd + 1) * 128], scalar=kTs[pr, g, j:j + 1],
                                                                                   in1=Ss[pr, g, :], op0=ALU.mult, op1=ALU.add), [pz[1], kTs, Ss], [Ss])
                    ST(ns_s[j, 2 * g:2 * g + 2, :, :].rearrange("h k v -> (h k) v"), Ss[:, g, :], [Ss])
                for hd in range(4):
                    g, pbase = hd // 2, (hd % 2) * 64
                    PE(lambda h, hd=hd, g=g, j=j: h.matmul(po[0:SB, hd * 128:(hd + 1) * 128], lhsT=qselZ[:, hd % 2, g, j, :], rhs=Ss[:, g, :],
                                                         start=(j == 0 and hd == 0), stop=(j == SB - 1)), [qselZ, Ss], [po])
            gla_outnorm(SB, po, srg, og_b)
            for hd in range(4):
                PE(lambda h, hd=hd: h.transpose(out=ptp[:, hd * SB:(hd + 1) * SB], in_=og_b[0:SB, hd * 128:(hd + 1) * 128], identity=ident_b[0:SB, 0:SB]), [og_b, ident_b], [ptp])
            V(lambda h: h.tensor_copy(out=ogT[:, :, 0:SB], in_=ptp[:, 0:4 * SB].rearrange("p (g t) -> p g t", g=4)), [ptp], [ogT])
            P.dma(sp, lambda h: h.dma_start(out=sc_og_s.rearrange("g p t -> p g t"), in_=ogT[:, :, 0:SB]), [ogT], [DB(d_og_s)])

            gl = []
            for j in range(SB):
                gl += [(cache_k, j * PAGES + pg) for pg in range(PAGES)]
                gl += [(cache_v, j * PAGES + pg) for pg in range(PAGES)]
            issued = [0]
            NB = len(kpg)

            def fetch(i):
                while issued[0] < len(gl) and issued[0] <= i + NB - 1:
                    k = issued[0]
                    cache, col = gl[k]
                    kb_ = kpg[k % NB]
                    P.dma(pool, lambda h, kb_=kb_, col=col, cache=cache: h.indirect_dma_start(
                        out=kb_[:], out_offset=None, in_=cache, in_offset=bass.IndirectOffsetOnAxis(ap=idx[:, col:col + 1], axis=0)), [idx], [kb_])
                    issued[0] += 1
                return kpg[i % NB]

            gi = [0]
            for j in range(SB):
                V(lambda h, j=j: h.tensor_copy(out=hT[:, :, :], in_=hTs[:, :, j:j + 1].to_broadcast([128, 8, 128])), [hTs], [hT])
                proj_tm(128, lambda kc: hT[:, kc, :], hT, w1a, W1A, 0, 512, pm[0])
                V(lambda h: h.tensor_scalar(out=qbc[:], in0=pm[0][:], scalar1=0.125, scalar2=None, op0=ALU.mult), [pm[0]], [qbc])
                for pg in range(PAGES):
                    kp = fetch(gi[0])
                    gi[0] += 1
                    pd = prod[0]
                    V(lambda h, kp=kp, pd=pd: h.tensor_tensor(out=pd[:], in0=kp[:], in1=qbc[:], op=ALU.mult), [kp, qbc], [pd])
                    V(lambda h, pd=pd, pg=pg: h.tensor_reduce(out=zt[:, pg, :], in_=pd[:].rearrange("p (h d) -> p h d", h=8), axis=AX.X, op=ALU.add), [pd], [zt])
                    yield
                V(lambda h: h.tensor_tensor(out=zt[:], in0=zt[:], in1=biasbc[:].unsqueeze(1).to_broadcast([128, PAGES, 8]), op=ALU.add), [zt, biasbc], [zt])
                ACT(lambda h: h.activation(out=As[:], in_=zt[:], func=AF.Exp), [zt], [As])
                ACT(lambda h: h.activation(out=sps[:], in_=As[:], func=AF.Ln, bias=1.0), [As], [sps])
                for hd in range(8):
                    PE(lambda h, hd=hd: h.matmul(pm[1][0:PAGES, hd:hd + 1], lhsT=sps[:, :, hd], rhs=negones[:, 0:1], start=(hd == 0), stop=(hd == 7)), [sps, negones], [pm[1]])
                V(lambda h: h.tensor_copy(out=tots[0:PAGES, :], in_=pm[1][0:PAGES, 0:8]), [pm[1]], [tots])
                V(lambda h: h.tensor_tensor(out=rhsE[0:PAGES, :, :], in0=tots[0:PAGES, :].unsqueeze(1).to_broadcast([PAGES, PAGES, 8]),
                                            in1=umat[0:PAGES, 0:PAGES].unsqueeze(2).to_broadcast([PAGES, PAGES, 8]), op=ALU.mult), [tots, umat], [rhsE])
                ncol = PAGES * 8
                spf = sps[:].rearrange("p a b -> p (a b)")
                ref = rhsE[:].rearrange("p a b -> p (a b)")
                ztf = zt[:].rearrange("p a b -> p (a b)")
                Asf = As[:].rearrange("p a b -> p (a b)")
                for c0 in range(0, ncol, 512):
                    cw = min(512, ncol - c0)
                    z = pz[(c0 // 512) % 2]
                    PE(lambda h, z=z, c0=c0, cw=cw: h.matmul(z[:, 0:cw], lhsT=negtri[:], rhs=spf[:, c0:c0 + cw], start=True, stop=False), [negtri, sps], [z])
                    PE(lambda h, z=z, c0=c0, cw=cw: h.matmul(z[:, 0:cw], lhsT=ones_b[0:PAGES, :], rhs=ref[0:PAGES, c0:c0 + cw], start=False, stop=True), [ones_b, rhsE], [z])
                    V(lambda h, z=z, c0=c0, cw=cw: h.tensor_tensor(out=ztf[:, c0:c0 + cw], in0=ztf[:, c0:c0 + cw], in1=z[:, 0:cw], op=ALU.add), [zt, z], [zt])
                ACT(lambda h: h.activation(out=As[:], in_=zt[:], func=AF.Exp), [zt], [As])
                for pg in range(PAGES):
                    kp = fetch(gi[0])
                    gi[0] += 1
                    PE(lambda h, kp=kp, pg=pg: h.matmul(pg2[0:8, :], lhsT=As[:, pg, :], rhs=kp[:], start=(pg == 0), stop=(pg == PAGES - 1)), [As, kp], [pg2])
                    yield
                V(lambda h: h.tensor_tensor(out=msk[:], in0=pg2[0:8, :], in1=bd8[:], op=ALU.mult), [pg2, bd8], [msk])
                PE(lambda h, j=j: h.matmul(pz[1][0:SB, :], lhsT=eye4[0:8, j, 0:SB], rhs=msk[:], start=True, stop=True), [eye4, msk], [pz[1]])
                V(lambda h: h.tensor_tensor(out=osb_acc[:], in0=osb_acc[:], in1=pz[1][0:SB, :], op=ALU.add), [osb_acc, pz[1]], [osb_acc])
            V(lambda h: h.tensor_copy(out=osb_tm_b[:], in_=osb_acc[:]), [osb_acc], [osb_tm_b])
            for hd in range(8):
                PE(lambda h, hd=hd: h.transpose(out=ptp[0:64, hd * SB:(hd + 1) * SB], in_=osb_tm_b[:, hd * 64:(hd + 1) * 64], identity=ident_b[0:SB, 0:SB]), [osb_tm_b, ident_b], [ptp])
            V(lambda h: h.tensor_copy(out=osb[:, :, 0:SB], in_=ptp[0:64, 0:8 * SB].rearrange("p (g t) -> p g t", g=8)), [ptp], [osb])
            P.dma(sp, lambda h: h.dma_start(out=sc_osb_s.rearrange("g p t -> p g t"), in_=osb[:, :, 0:SB]), [osb], [DB(d_osb_s)])


        stop_at(1)
        r_ts = [r_t, T(xn[:, 0:512], xn.b)]
        osTs = [osT, T(xn[0:64, 512:1024], xn.b)]
        obank = [po, pg1]

        def attention_chunk(s, ch):
            tok0 = s * SEQ + ch * 512
            kmax = ch * 4 + 3
            for hp in range(0, 8, 2):
                heads = [(0, hp), (1, hp + 1)]
                for u, hd in heads:
                    V(lambda h, u=u: h.memset(r_ts[u][:], 0.0), [], [r_ts[u]])
                for i, kb in enumerate(range(kmax, -1, -1)):
                    c0 = max(0, (kb - ch * 4)) * 128
                    diag = kb >= ch * 4
                    for u, hd in heads:
                        cc = hd // 2
                        z = pz[u]
                        PE(lambda h, hd=hd, cc=cc, z=z, kb=kb, c0=c0: h.matmul(z[:, c0:512], lhsT=kT[:, cc, kb * 128:(kb + 1) * 128],
                                                                            rhs=qTz[:, hd % 2, cc, c0:512], start=True, stop=False), [kT, qTz], [z])
                        if diag:
                            PE(lambda h, z=z, c0=c0: h.matmul(z[:, c0:c0 + 128], lhsT=ident_b[:], rhs=negm[:], start=False, stop=False),
                               [ident_b, negm], [z])
                    for u, hd in heads:
                        z, e_, sp_ = pz[u], e_t[u], sp_t[u]
                        ACT(lambda h, hd=hd, z=z, e_=e_, c0=c0: h.activation(out=e_[:, c0:512], in_=z[:, c0:512], func=AF.Exp, bias=biasbc[:, hd:hd + 1]),
                            [z, biasbc], [e_])
                        ACT(lambda h, e_=e_, sp_=sp_, c0=c0: h.activation(out=sp_[:, c0:512], in_=e_[:, c0:512], func=AF.Ln, bias=1.0), [e_], [sp_])
                    for u, hd in heads:
                        z, sp_, r_ = pz[u], sp_t[u], r_ts[u]
                        PE(lambda h, z=z, sp_=sp_, c0=c0: h.matmul(z[:, c0:512], lhsT=negtri[:], rhs=sp_[:, c0:512], start=False, stop=False),
                           [negtri, sp_], [z])
                        PE(lambda h, z=z, r_=r_, c0=c0: h.matmul(z[:, c0:512], lhsT=negones[:], rhs=r_[:, c0:512], start=False, stop=True),
                           [negones, r_], [z])
                    for u, hd in heads:
                        z, a_, sp_, r_ = pz[u], a_t[u], sp_t[u], r_ts[u]
                        ACT(lambda h, hd=hd, z=z, a_=a_, c0=c0: h.activation(out=a_[:, c0:512], in_=z[:, c0:512], func=AF.Exp, bias=biasbc[:, hd:hd + 1]),
                            [z, biasbc], [a_])
                        if kb > 0:
                            V(lambda h, sp_=sp_, r_=r_, c0=c0: h.tensor_tensor(out=r_[:, c0:512], in0=r_[:, c0:512], in1=sp_[:, c0:512], op=ALU.add),
                              [r_, sp_], [r_])
                    for u, hd in heads:
                        a_, ob = a_t[u], obank[u]
                        PE(lambda h, hd=hd, a_=a_, ob=ob, kb=kb, c0=c0, i=i: h.matmul(ob[0:64, c0:512], lhsT=vseq[:, kb, hd * 64:(hd + 1) * 64], rhs=a_[:, c0:512],
                                                                                   start=(i == 0), stop=(kb == 0)), [vseq, a_], [ob])
                    pump(3)
                for u, hd in heads:
                    ob, ot = obank[u], osTs[u]
                    V(lambda h, ob=ob, ot=ot: h.tensor_copy(out=ot[:], in_=ob[0:64, :]), [ob], [ot])
                    P.dma(sp, lambda h, hd=hd, ot=ot: h.dma_start(out=sc_osb[hd, :, tok0:tok0 + 512], in_=ot[:]), [ot], [DB(d_osb)])

        def gla_tile(s, tl, tok0, sr):
            PE(lambda h: h.matmul(pg1[:, 0:256], lhsT=blktri[:], rhs=lpos[:], start=True, stop=True), [blktri, lpos], [pg1])
            ACT(lambda h: h.activation(out=eb[:], in_=pg1[:, 0:256], func=AF.Exp, scale=-1.0 / 16), [pg1], [eb])
            ACT(lambda h: h.activation(out=einv[:], in_=pg1[:, 0:256], func=AF.Exp, scale=1.0 / 16), [pg1], [einv])
            for g in range(2):
                PE(lambda h, g=g: h.matmul(pg1[:, 256 + 2 * g:258 + 2 * g], lhsT=lpos[:, g * 128:(g + 1) * 128], rhs=ind2[:], start=True, stop=True),
                   [lpos, ind2], [pg1])
            ACT(lambda h: h.activation(out=dec[:], in_=pg1[:, 256:260].rearrange("p (g c) -> p g c", g=2), func=AF.Exp, scale=-1.0 / 16), [pg1], [dec])
            stop_at(1.41)
            V(lambda h: h.scalar_tensor_tensor(out=qdec[:], in0=pm[1][:, 0:256], scalar=0.125, in1=eb[:], op0=ALU.mult, op1=ALU.mult), [pm[1], eb], [qdec])
            V(lambda h: h.tensor_tensor(out=kinv[:], in0=pm[1][:, 256:512], in1=einv[:], op=ALU.mult), [pm[1], einv], [kinv])
            for g in range(2):
                PE(lambda h, g=g: h.transpose(out=ptp[:, g * 128:(g + 1) * 128], in_=qdec[:, g * 128:(g + 1) * 128], identity=ident_b[:]), [qdec, ident_b], [ptp])
                PE(lambda h, g=g: h.transpose(out=ptp[:, 256 + g * 128:256 + (g + 1) * 128], in_=kinv[:, g * 128:(g + 1) * 128], identity=ident_b[:]), [kinv, ident_b], [ptp])
            V(lambda h: h.tensor_copy(out=qdT[:], in_=ptp[:, 0:256].rearrange("p (g t) -> p g t", g=2)), [ptp], [qdT])
            V(lambda h: h.tensor_copy(out=kiT[:], in_=ptp[:, 256:512].rearrange("p (g t) -> p g t", g=2)), [ptp], [kiT])
            stop_at(1.42)
            V(lambda h: h.tensor_copy(out=qdZ[0:64, 0, :, :], in_=qdT[0:64, :, :]), [qdT], [qdZ])
            V(lambda h: h.tensor_copy(out=qdZ[64:128, 1, :, :], in_=qdT[64:128, :, :]), [qdT], [qdZ])
            for hd in range(4):
                g, par = hd // 2, hd % 2
                PE(lambda h, hd=hd, g=g, par=par: h.matmul(pz[0][:, hd * 128:(hd + 1) * 128], lhsT=kiT[:, g, :], rhs=qdZ[:, par, g, :],
                                                         start=True, stop=True), [kiT, qdZ], [pz[0]])
            V(lambda h: h.tensor_tensor(out=attT[:].rearrange("p (h i) -> p h i", h=4), in0=pz[0][:].rearrange("p (h i) -> p h i", h=4),
                                        in1=blktri[:].unsqueeze(1).to_broadcast([128, 4, 128]), op=ALU.mult), [pz[0], blktri], [attT])
            stop_at(1.43)
            for hd in range(4):
                PE(lambda h, hd=hd: h.matmul(po[:, hd * 128:(hd + 1) * 128], lhsT=attT[:, hd * 128:(hd + 1) * 128], rhs=gvb[:, hd * 128:(hd + 1) * 128],
                                             start=(hd == 0), stop=False), [attT, gvb], [po])
            stop_at(1.44)
            for par in range(2):
                pr = slice(par * 64, (par + 1) * 64)
                for c in range(2):
                    V(lambda h, par=par, pr=pr, c=c: h.tensor_copy(out=qdZ2[pr, par, :, c, c * 64:(c + 1) * 64], in_=qdT[pr, :, c * 64:(c + 1) * 64]), [qdT], [qdZ2])
            for c in range(2):
                V(lambda h, c=c: h.tensor_copy(out=kinvZ[c * 64:(c + 1) * 64, c, :], in_=kinv[c * 64:(c + 1) * 64, :]), [kinv], [kinvZ])
            for c in range(2):
                for hd in range(4):
                    g, par = hd // 2, hd % 2
                    PE(lambda h, hd=hd, g=g, par=par, c=c: h.matmul(po[:, hd * 128:(hd + 1) * 128], lhsT=qdZ2[:, par, g, c, :],
                                                                  rhs=S16[:, g, :], start=False, stop=(c == 1)),
                       [qdZ2, S16], [po])
                for g in range(2):
                    PE(lambda h, g=g, c=c: h.matmul(pg1[:, g * 256:(g + 1) * 256], lhsT=kinvZ[:, c, g * 128:(g + 1) * 128], rhs=gvb[:, g * 256:(g + 1) * 256],
                                                   start=True, stop=True), [kinvZ, gvb], [pg1])
                kvv = pg1[:].rearrange("p (g x) -> p g x", g=2)
                V(lambda h, kvv=kvv: h.tensor_tensor(out=S32[0:64, :, :], in0=S32[0:64, :, :], in1=kvv[0:64, :, 0:128], op=ALU.add), [S32, pg1], [S32])
                V(lambda h, kvv=kvv: h.tensor_tensor(out=S32[64:128, :, :], in0=S32[64:128, :, :], in1=kvv[64:128, :, 128:256], op=ALU.add), [S32, pg1], [S32])
                V(lambda h, c=c: h.tensor_tensor(out=S32[:], in0=S32[:], in1=dec[:, :, c:c + 1].to_broadcast([128, 2, 128]), op=ALU.mult), [S32, dec], [S32])
                V(lambda h: h.tensor_copy(out=S16[:], in_=S32[:]), [S32], [S16])
            stop_at(1.45)
            gla_outnorm(128, po, sr, og_b)
            stop_at(1.46)
            for hd in range(4):
                PE(lambda h, hd=hd: h.transpose(out=ptp[:, hd * 128:(hd + 1) * 128], in_=og_b[:, hd * 128:(hd + 1) * 128], identity=ident_b[:]), [og_b, ident_b], [ptp])
            V(lambda h: h.tensor_copy(out=ogT[:], in_=ptp[:, 0:512].rearrange("p (g t) -> p g t", g=4)), [ptp], [ogT])
            P.dma(sp, lambda h: h.dma_start(out=sc_og[:, :, tok0:tok0 + 128].rearrange("g p t -> p g t"), in_=ogT[:]), [ogT], [DB(d_og)])

        tile_ctr = [0]
        sflow = sample_flow()
        sdone = [False]

        def pump(k):
            for _ in range(k):
                if sdone[0]:
                    return
                try:
                    next(sflow)
                except StopIteration:
                    sdone[0] = True

        for s in range(PB):
            V(lambda h: h.memset(S32[:], 0.0), [], [S32])
            V(lambda h: h.memset(S16[:], 0.0), [], [S16])
            for ch in range(NCH):
                for tl in range(4):
                    ti = ch * 4 + tl
                    tok0 = s * SEQ + ti * 128
                    xb = xt[tile_ctr[0] % 2]
                    tile_ctr[0] += 1
                    LD(xb[:], x_p[tok0:tok0 + 128, :], [xb])
                    hb, hv = ln_to_hT(xb, 128, s, 0, 8)
                    for grp in range(2):
                        for c4 in range(4):
                            cc = grp * 4 + c4
                            for kc in range(8):
                                PE(lambda h, kc=kc, cc=cc, c4=c4, grp=grp: h.matmul(pm[grp][:, c4 * 128:(c4 + 1) * 128], lhsT=w1a[:, kc, cc * 128:(cc + 1) * 128],
                                                                                 rhs=hT[:, kc, :], start=(kc == 0), stop=(kc == 7)), [W1A, hb], [pm[grp]])
                    ACT(lambda h, tl=tl: h.activation(out=qT[:, :, tl * 128:(tl + 1) * 128], in_=pm[0][:].rearrange("p (c t) -> p c t", c=4), func=AF.Copy, scale=0.125),
                        [pm[0]], [qT])
                    V(lambda h, ti=ti: h.tensor_copy(out=kT[:, :, ti * 128:(ti + 1) * 128], in_=pm[1][:].rearrange("p (c t) -> p c t", c=4)), [pm[1]], [kT])
                    V(lambda h, tl=tl: h.tensor_copy(out=qTz[0:64, 0, :, tl * 128:(tl + 1) * 128], in_=qT[0:64, :, tl * 128:(tl + 1) * 128]), [qT], [qTz])
                    V(lambda h, tl=tl: h.tensor_copy(out=qTz[64:128, 1, :, tl * 128:(tl + 1) * 128], in_=qT[64:128, :, tl * 128:(tl + 1) * 128]), [qT], [qTz])
                    proj_tm(128, hv, hb, w1a, W1A, 512, 512, pm[0])
                    V(lambda h: h.tensor_copy(out=kst[:], in_=pm[0][:]), [pm[0]], [kst])
                    stop_at(1.21)
                    ST(nk_p[tok0:tok0 + 128, :], kst[:], [kst])
                    stop_at(1.22)
                    proj_tm(128, hv, hb, w1a, W1A, 1024, 512, pm[1])
                    V(lambda h: h.tensor_copy(out=vst[:], in_=pm[1][:]), [pm[1]], [vst])
                    stop_at(1.23)
                    V(lambda h, ti=ti: h.tensor_copy(out=vseq[:, ti, :], in_=vst[:]), [vst], [vseq])
                    stop_at(1.24)
                    ST(nv_p[tok0:tok0 + 128, :], vst[:], [vst])
                    proj_tm(128, hv, hb, w1a, W1A, 2048, 512, pm[0])
                    V(lambda h: h.tensor_copy(out=gvb[:], in_=pm[0][:]), [pm[0]], [gvb])
                    proj_tm(128, hv, hb, w1a, W1A, 2560, 512, pm[0])
                    ACT(lambda h: h.activation(out=srg[:], in_=pm[0][:], func=AF.Silu), [pm[0]], [srg])
                    gla_logdecay(128, hv, hb)
                    proj_tm(128, hv, hb, w1a, W1A, 1536, 512, pm[1])
                    gla_tile(s, tl, tok0, srg)
                    pump(2)
                attention_chunk(s, ch)
            for g in range(2):
                ST(ns_p[s, 2 * g:2 * g + 2, :, :].rearrange("h k v -> (h k) v"), S32[:, g, :], [S32])

        stop_at(2)
        pump(10 ** 9)

        stop_at(4)
        P.barrier()
        off[0] = 0
        limit[0] = RAWN - TOPN
        LD(lng[:], ln1g.partition_broadcast(128), [lng])
        LD(lnb[:], ln1b.partition_broadcast(128), [lnb])
        wg_t = sb([128, 8, 2048], BF16)
        wua_t = sb([128, 8, 1024], BF16)
        wub_t = sb([128, 4, 1024], BF16)
        wo_t = sb([128, 8, 1024], BF16)
        wg, wua, wub, wo = wg_t.t, wua_t.t, wub_t.t, wo_t.t
        WG, WU = wg_t, wua_t
        gate = sb([128, 16, 128], BF16)
        t1 = sb([128, 512], F32)
        mT = sb([128, 8, 128], BF16)
        for kc in range(8):
            CASTLD(wg[:, kc, :], w_in_v[:, kc, N1A:NIN], [WG])
        w_upa_v = w_upa.rearrange("(h p) n -> p h n", p=64)
        for h_ in range(8):
            CASTLD(wua[0:64, h_, :], w_upa_v[:, h_, :], [WU])
        w_upb_v = w_upb.rearrange("(k p) n -> p k n", p=128)
        for kc in range(4):
            CASTLD(wub[:, kc, :], w_upb_v[:, kc, :], [WU])
        w_o_v = w_o.rearrange("(k p) n -> p k n", p=128)
        for kc in range(8):
            CASTLD(wo[:, kc, :], w_o_v[:, kc, :], [WU])
        wf1_t = sb([128, 8, 4096], BF16)
        wf1 = wf1_t.t
        WF1 = wf1_t
        w_ff1_v = w_ff1.rearrange("(k p) n -> p k n", p=128)
        for kc in range(8):
            for c in range(0, 4096, 2048):
                CASTLD(wf1[:, kc, c:c + 2048], w_ff1_v[:, kc, c:c + 2048], [WF1])
        def phase1b_front(n, xsrc_ap, r, osb_src, og_src, dbufs_in):
            par = tile_ctr[0] % 2
            xb = xt[par]
            tile_ctr[0] += 1
            LD(xb[0:n, :], xsrc_ap, [xb])
            hb, hv = ln_to_hT(xb, n, r, 0, 8, hT=(hT, hT2)[par])
            osb_, ogl_ = (osb, osb2)[par], (ogl, ogl2)[par]
            P.dma(sp, lambda h: h.dma_start(out=osb_[:, :, 0:n], in_=osb_src), [DB(dbufs_in[0])], [osb_])
            P.dma(sp, lambda h: h.dma_start(out=ogl_[:, :, 0:n], in_=og_src), [DB(dbufs_in[1])], [ogl_])
            return xb, hb, hv, osb_, ogl_

        gate_tm = gate[:].rearrange("p c t -> p (c t)")

        def phase1b_back(ctx, n, gt, x1_dst_ap, dbuf_out):
            xb, hb, hv, osb, ogl = ctx
            for blk in range(4):
                z = pm[blk % 2]
                for kc in range(8):
                    PE(lambda h, z=z, kc=kc, blk=blk: h.matmul(z[0:n, :], lhsT=hv(kc)[:, 0:n], rhs=wg[:, kc, blk * 512:(blk + 1) * 512],
                                                            start=(kc == 0), stop=(kc == 7)), [WG, hb], [z])
                ACT(lambda h, z=z, blk=blk: h.activation(out=gate_tm[0:n, blk * 512:(blk + 1) * 512], in_=z[0:n, :], func=AF.Sigmoid), [z], [gate])
            for half in range(2):
                cs = slice(half * 512, (half + 1) * 512)
                for hd in range(8):
                    PE(lambda h, hd=hd, cs=cs: h.matmul(pz[0][0:n, :], lhsT=osb[:, hd, 0:n], rhs=wua[0:64, hd, cs], start=(hd == 0), stop=(hd == 7)),
                       [WU, osb], [pz[0]])
                for kc in range(4):
                    PE(lambda h, kc=kc, cs=cs: h.matmul(pz[1][0:n, :], lhsT=ogl[:, kc, 0:n], rhs=wub[:, kc, cs], start=(kc == 0), stop=(kc == 3)),
                       [WU, ogl], [pz[1]])
                V(lambda h, cs=cs: h.tensor_tensor(out=big[0:n, cs], in0=pz[0][0:n, :], in1=gate_tm[0:n, cs], op=ALU.mult), [pz[0], gate], [big])
                V(lambda h, half=half: h.tensor_tensor(out=t1[0:n, :], in0=pz[1][0:n, :], in1=gate_tm[0:n, 1024 + half * 512:1024 + (half + 1) * 512], op=ALU.mult),
                  [pz[1], gate], [t1])
                V(lambda h, cs=cs: h.tensor_tensor(out=xn[0:n, cs], in0=big[0:n, cs], in1=t1[0:n, :], op=ALU.add), [big, t1], [xn])
            for c in range(8):
                PE(lambda h, c=c: h.transpose(out=ptp[:, c * 128:c * 128 + n], in_=xn[0:n, c * 128:(c + 1) * 128], identity=ident_b[0:n, 0:n]),
                   [xn, ident_b], [ptp])
            V(lambda h: h.tensor_copy(out=mT[:, :, 0:n], in_=ptp[:, :].rearrange("p (c t) -> p c t", c=8)[:, :, 0:n]), [ptp], [mT])
            for half in range(2):
                for kc in range(8):
                    PE(lambda h, kc=kc, half=half: h.matmul(pm[half][0:n, :], lhsT=mT[:, kc, 0:n], rhs=wo[:, kc, half * 512:(half + 1) * 512], start=(kc == 0), stop=(kc == 7)),
                       [mT, WU], [pm[half]])
            residual_ln(n, xb, pm[0], pm[1], gt, big2)
            P.dma(sp, lambda h: h.dma_start(out=x1_dst_ap, in_=big2[0:n, :]), [big2], [DB(dbuf_out)])

        jobs = []
        for s in range(PB):
            for ti in range(NT_SEQ):
                tok0 = s * SEQ + ti * 128
                jobs.append((s if ti == 0 else None,
                             (128, x_p[tok0:tok0 + 128, :], s, sc_osb[:, :, tok0:tok0 + 128].rearrange("g p t -> p g t"),
                              sc_og[:, :, tok0:tok0 + 128].rearrange("g p t -> p g t"), (d_osb, d_og)),
                             (128, gtbc, sc_x1[tok0:tok0 + 128, :], d_x1)))
        jobs.append(("s", (SB, x_s, None, sc_osb_s.rearrange("g p t -> p g t"), sc_og_s.rearrange("g p t -> p g t"), (d_osb_s, d_og_s)),
                     (SB, gts, sc_x1_s, d_x1_s)))
        ctx = phase1b_front(*jobs[0][1])
        for i, (mk, fa, ba) in enumerate(jobs):
            nxt = phase1b_front(*jobs[i + 1][1]) if i + 1 < len(jobs) else None
            if mk == "s":
                make_gt_s(16)
            elif mk is not None:
                make_gt_bc(mk, 16)
            phase1b_back(ctx, *ba)
            ctx = nxt

        stop_at(5)
        P.barrier()
        off[0] = 0
        wf2_t = sb([128, 32, 1024], BF16)
        wf2 = wf2_t.t
        WF = wf2_t
        fT = sb([128, 32, 128], BF16)
        rl = sb([128, 512], BF16)
        w_ff2_v = w_ff2.rearrange("(k p) n -> p k n", p=128)
        for kc in range(32):
            CASTLD(wf2[:, kc, :], w_ff2_v[:, kc, :], [WF])
        LD(lng[:], ln2g.partition_broadcast(128), [lng])
        LD(lnb[:], ln2b.partition_broadcast(128), [lnb])

        def phase2_front(n, xsrc_ap, dbuf_in, r):
            par = tile_ctr[0] % 2
            xb = xt[par]
            tile_ctr[0] += 1
            P.dma(sp, lambda h: h.dma_start(out=xb[0:n, :], in_=xsrc_ap), [DB(dbuf_in)], [xb])
            hb, hv = ln_to_hT(xb, n, r, 24, 32, hT=(hT, hT2)[par])
            return xb, hb, hv

        def phase2_back(ctx, n, gt, y_dst_ap):
            xb, hb, hv = ctx
            for grp in range(8):
                z = pz[grp % 2]
                for c4 in range(4):
                    cc = grp * 4 + c4
                    for kc in range(8):
                        PE(lambda h, z=z, kc=kc, cc=cc, c4=c4: h.matmul(z[:, c4 * 128:c4 * 128 + n], lhsT=wf1[:, kc, cc * 128:(cc + 1) * 128], rhs=hv(kc)[:, 0:n],
                                                                     start=(kc == 0), stop=(kc == 7)), [WF1, hb], [z])
                rlv = rl[:].rearrange("p (c t) -> p c t", c=4)[:, :, 0:n]
                ACT(lambda h, z=z, rlv=rlv: h.activation(out=rlv, in_=z[:].rearrange("p (c t) -> p c t", c=4)[:, :, 0:n], func=AF.Relu), [z], [rl])
                V(lambda h, grp=grp, rlv=rlv: h.tensor_tensor(out=fT[:, grp * 4:(grp + 1) * 4, 0:n], in0=rlv, in1=rlv, op=ALU.mult), [rl], [fT])
            for half in range(2):
                for kc in range(32):
                    PE(lambda h, kc=kc, half=half: h.matmul(pm[half][0:n, :], lhsT=fT[:, kc, 0:n], rhs=wf2[:, kc, half * 512:(half + 1) * 512], start=(kc == 0), stop=(kc == 31)),
                       [fT, WF], [pm[half]])
            residual_ln(n, xb, pm[0], pm[1], gt, big2)
            ST(y_dst_ap, big2[0:n, :], [big2])

        jobs = []
        for s in range(PB):
            for ti in range(NT_SEQ):
                tok0 = s * SEQ + ti * 128
                jobs.append((s if ti == 0 else None, (128, sc_x1[tok0:tok0 + 128, :], d_x1, s), (128, gtbc, y_p[tok0:tok0 + 128, :])))
        jobs.append(("s", (SB, sc_x1_s, d_x1_s, None), (SB, gts, y_s)))
        ctx = phase2_front(*jobs[0][1])
        for i, (mk, fa, ba) in enumerate(jobs):
            nxt = phase2_front(*jobs[i + 1][1]) if i + 1 < len(jobs) else None
            if mk == "s":
                make_gt_s(40)
            elif mk is not None:
                make_gt_bc(mk, 40)
            phase2_back(ctx, *ba)
            ctx = nxt

    except StopBuild:
        pass

    sems = []
    import contextlib
    with contextlib.ExitStack() as es:
        for e in P.engs:
            e.sem = es.enter_context(nc.semaphore("s_" + e.name))
        for d_ in P.dsems:
            d_.sem = es.enter_context(nc.semaphore(d_.name))
        block = es.enter_context(nc.Block())
        P.emit(block)
    return nc


_CACHE = {}


def kernel(x_prompt, x_sample, cache_k, cache_v, state_gla, page_table, c_prompt, c_sample,
           w_ada, b_ada, w_in, sb_bias, w_gla_g2, b_gla_g, gla_norm_g, w_up_a, w_up_b, w_o,
           ln1_g, ln1_b, w_ff1, w_ff2, ln2_g, ln2_b, _cores=NCORES):
    f = lambda a: np.ascontiguousarray(np.asarray(a))
    B, SEQ, _ = x_prompt.shape
    DB_ = x_sample.shape[0]
    nphys = cache_k.shape[1]
    pages = page_table.shape[1]
    pb, sbn = B // _cores, DB_ // _cores
    cfg = Cfg(pb=pb, seq=SEQ, sb=sbn, pages=pages, nphys=nphys)
    key = (pb, SEQ, sbn, pages, nphys)
    if key not in _CACHE:
        _CACHE[key] = build(cfg)
    nc = _CACHE[key]
    ck = f(cache_k).reshape(nphys * 128, 512)
    cv = f(cache_v).reshape(nphys * 128, 512)
    shared = {
        "cache_k": ck, "cache_v": cv, "w_ada": f(w_ada[0]), "b_ada": f(b_ada[0]).reshape(1, -1), "w_in": f(w_in[0]),
        "sb_bias": f(sb_bias[0]).reshape(1, 8), "w_g2": f(w_gla_g2[0]), "b_g": f(b_gla_g[0]).reshape(1, 256),
        "normg": f(gla_norm_g[0]).reshape(1, 128), "w_upa": f(w_up_a[0]), "w_upb": f(w_up_b[0]), "w_o": f(w_o[0]),
        "ln1g": f(ln1_g[0]).reshape(1, D), "ln1b": f(ln1_b[0]).reshape(1, D), "w_ff1": f(w_ff1[0]), "w_ff2": f(w_ff2[0]),
        "ln2g": f(ln2_g[0]).reshape(1, D), "ln2b": f(ln2_b[0]).reshape(1, D),
    }
    in_maps = []
    for c in range(_cores):
        m = dict(shared)
        m["x_p"] = f(x_prompt[c * pb:(c + 1) * pb]).reshape(pb * SEQ, D)
        m["x_s"] = f(x_sample[c * sbn:(c + 1) * sbn]).reshape(sbn, D)
        m["state"] = f(state_gla[0, c * sbn:(c + 1) * sbn])
        m["ptab"] = f(page_table[c * sbn:(c + 1) * sbn]).reshape(1, sbn * pages).astype(np.int32)
        m["c_all"] = np.concatenate([f(c_prompt[c * pb:(c + 1) * pb]), f(c_sample[c * sbn:(c + 1) * sbn])], axis=0)
        in_maps.append(m)
    res = run_bass_kernel_spmd(nc, in_maps, core_ids=list(range(_cores)))
    R = res.results
    cat = lambda k: np.concatenate([np.asarray(r[k]) for r in R], axis=0)
    y_p = cat("y_p").reshape(B, SEQ, D)
    y_s = cat("y_s").reshape(DB_, 1, D)
    nk_p = cat("nk_p").reshape(1, B, SEQ, 8, 64)
    nv_p = cat("nv_p").reshape(1, B, SEQ, 8, 64)
    ns_p = cat("ns_p").reshape(1, B, 4, 64, 128)
    nk_s = cat("nk_s").reshape(1, DB_, 1, 8, 64)
    nv_s = cat("nv_s").reshape(1, DB_, 1, 8, 64)
    ns_s = cat("ns_s").reshape(1, DB_, 4, 64, 128)
    return tuple(np.ascontiguousarray(a, dtype=np.float32) for a in (y_p, y_s, nk_p, nv_p, ns_p, nk_s, nv_s, ns_s))
```
